# Optimizing a Trainium2 kernel written in Bass

```python
import jax, jax.numpy as jnp
from jax import lax
import numpy as np

D_MODEL = 1024
BATCH = 16
SEQ = 4096
DEPTH = 1
DEC_BATCH = 1
DEC_SEQ = 16384
PAST_LEN = 128

D_RNN = 1024
N_RNN_BLOCKS = 8
RNN_BLOCK = D_RNN // N_RNN_BLOCKS
RNN_CONV_W = 4
LRU_C = 8.0
D_CONV = 1024
CONF_CONV_W = 31
D_FF = -(-8 * D_MODEL // (3 * 256)) * 256
IN_COLS = 2 * D_RNN + 2 * D_CONV + 2 * D_MODEL
LN_EPS = 1e-5
DEEPNORM_ALPHA = (2.0 * DEPTH) ** 0.25
DEEPNORM_BETA = (8.0 * DEPTH) ** -0.25

kernel_name = "hybrid_rglru_conformer_encoder"


def layer_norm(x, g, b):
    xf = x.astype(jnp.float32)
    mu = jnp.mean(xf, axis=-1, keepdims=True)
    var = jnp.mean(jnp.square(xf - mu), axis=-1, keepdims=True)
    y = (xf - mu) * lax.rsqrt(var + LN_EPS) * g.astype(jnp.float32) + b.astype(jnp.float32)
    return y.astype(x.dtype)


def depthwise_conv(x, w, b, pad_left, pad_right):
    c = x.shape[-1]
    y = lax.conv_general_dilated(
        x, w.astype(x.dtype)[:, None, :], window_strides=(1,),
        padding=[(pad_left, pad_right)],
        dimension_numbers=("NWC", "WIO", "NWC"), feature_group_count=c)
    return y + b.astype(x.dtype)


def linear_recurrence(a, b, reverse):
    def combine(left, right):
        a_l, b_l = left
        a_r, b_r = right
        return a_l * a_r, a_r * b_l + b_r
    _, h = lax.associative_scan(combine, (a, b), axis=1, reverse=reverse)
    return h


def rg_lru(xc, wa, ba, wx, bx, lam, reverse):
    bsz, s, _ = xc.shape
    xb = xc.reshape(bsz, s, N_RNN_BLOCKS, RNN_BLOCK)
    r = jax.nn.sigmoid(jnp.einsum("bshi,hij->bshj", xb, wa.astype(jnp.float32)).reshape(bsz, s, D_RNN)
                       + ba.astype(jnp.float32))
    i = jax.nn.sigmoid(jnp.einsum("bshi,hij->bshj", xb, wx.astype(jnp.float32)).reshape(bsz, s, D_RNN)
                       + bx.astype(jnp.float32))
    log_a = -LRU_C * r * jax.nn.softplus(-lam.astype(jnp.float32))
    a = jnp.exp(log_a)
    b = jnp.sqrt(-jnp.expm1(2.0 * log_a)) * (i * xc)
    return linear_recurrence(a, b, reverse)


def encoder_layer(x, w_in, rnn_conv_w, rnn_conv_b, lru_wa, lru_ba, lru_wx, lru_bx, lru_lambda,
                  w_rnn_proj, conf_conv_w, conf_conv_b, conf_norm_g, conf_norm_b, w_conf_proj,
                  gate_b, w_out, ln1_g, ln1_b, w_ffn_in, w_ffn_out, ln2_g, ln2_b):
    bsz, s, _ = x.shape
    proj = x @ w_in
    x_rnn, g_rnn, conf_u, conf_v, gate_logits = jnp.split(
        proj, [D_RNN, 2 * D_RNN, 2 * D_RNN + D_CONV, 2 * D_RNN + 2 * D_CONV], axis=-1)

    pad_l = RNN_CONV_W // 2
    xc = depthwise_conv(x_rnn, rnn_conv_w, rnn_conv_b, pad_l, RNN_CONV_W - 1 - pad_l).astype(jnp.float32)
    h = (rg_lru(xc, lru_wa[0], lru_ba[0], lru_wx[0], lru_bx[0], lru_lambda[0], reverse=False)
         + rg_lru(xc, lru_wa[1], lru_ba[1], lru_wx[1], lru_bx[1], lru_lambda[1], reverse=True))
    y_a = (h.astype(x.dtype) * jax.nn.gelu(g_rnn)) @ w_rnn_proj

    u = conf_u * jax.nn.sigmoid(conf_v)
    pad_c = CONF_CONV_W // 2
    u = depthwise_conv(u, conf_conv_w, conf_conv_b, pad_c, CONF_CONV_W - 1 - pad_c)
    u = jax.nn.silu(layer_norm(u, conf_norm_g, conf_norm_b))
    y_b = u @ w_conf_proj

    gates = jax.nn.sigmoid(gate_logits.reshape(bsz, s, 2, D_MODEL) + gate_b)
    merged = gates[:, :, 0, :] * y_a + gates[:, :, 1, :] * y_b
    x = layer_norm(DEEPNORM_ALPHA * x + merged @ w_out, ln1_g, ln1_b)

    g_ff, up = jnp.split(x @ w_ffn_in, [D_FF], axis=-1)
    x = layer_norm(DEEPNORM_ALPHA * x + (jax.nn.silu(g_ff) * up) @ w_ffn_out, ln2_g, ln2_b)
    return x


def trunk(x, weights):
    for l in range(DEPTH):
        x = encoder_layer(x, *[w[l] for w in weights])
    return x


def setup_inputs(seed: int = 0) -> dict:
    key = jax.random.key(seed)
    ks = jax.random.split(key, 24)
    f32 = jnp.float32

    def nrm(k, shape, scale):
        return jax.random.normal(k, shape, f32) * scale

    a0 = jax.random.uniform(ks[8], (DEPTH, 2, D_RNN), f32, 0.9, 0.999)
    p = a0 ** (1.0 / LRU_C)
    lru_lambda = jnp.log(p) - jnp.log1p(-p)
    return {
        "x_prompt": nrm(ks[0], (BATCH, SEQ, D_MODEL), 1.0),
        "x_sample": nrm(ks[1], (DEC_BATCH, DEC_SEQ, D_MODEL), 1.0),
        "w_in": nrm(ks[2], (DEPTH, D_MODEL, IN_COLS), D_MODEL ** -0.5),
        "rnn_conv_w": nrm(ks[3], (DEPTH, RNN_CONV_W, D_RNN), RNN_CONV_W ** -0.5),
        "rnn_conv_b": nrm(ks[4], (DEPTH, D_RNN), 0.01),
        "lru_wa": nrm(ks[5], (DEPTH, 2, N_RNN_BLOCKS, RNN_BLOCK, RNN_BLOCK), RNN_BLOCK ** -0.5),
        "lru_ba": nrm(ks[6], (DEPTH, 2, D_RNN), 0.01),
        "lru_wx": nrm(ks[7], (DEPTH, 2, N_RNN_BLOCKS, RNN_BLOCK, RNN_BLOCK), RNN_BLOCK ** -0.5),
        "lru_bx": nrm(ks[9], (DEPTH, 2, D_RNN), 0.01),
        "lru_lambda": lru_lambda,
        "w_rnn_proj": nrm(ks[10], (DEPTH, D_RNN, D_MODEL), D_RNN ** -0.5),
        "conf_conv_w": nrm(ks[11], (DEPTH, CONF_CONV_W, D_CONV), CONF_CONV_W ** -0.5),
        "conf_conv_b": nrm(ks[12], (DEPTH, D_CONV), 0.01),
        "conf_norm_g": 1.0 + nrm(ks[13], (DEPTH, D_CONV), 0.02),
        "conf_norm_b": nrm(ks[14], (DEPTH, D_CONV), 0.01),
        "w_conf_proj": nrm(ks[15], (DEPTH, D_CONV, D_MODEL), D_CONV ** -0.5),
        "gate_b": nrm(ks[16], (DEPTH, 2, D_MODEL), 0.01),
        "w_out": nrm(ks[17], (DEPTH, D_MODEL, D_MODEL), D_MODEL ** -0.5 * DEEPNORM_BETA),
        "ln1_g": 1.0 + nrm(ks[18], (DEPTH, D_MODEL), 0.02),
        "ln1_b": nrm(ks[19], (DEPTH, D_MODEL), 0.01),
        "w_ffn_in": nrm(ks[20], (DEPTH, D_MODEL, 2 * D_FF), D_MODEL ** -0.5),
        "w_ffn_out": nrm(ks[21], (DEPTH, D_FF, D_MODEL), D_FF ** -0.5 * DEEPNORM_BETA),
        "ln2_g": 1.0 + nrm(ks[22], (DEPTH, D_MODEL), 0.02),
        "ln2_b": nrm(ks[23], (DEPTH, D_MODEL), 0.01),
    }


def reference(x_prompt, x_sample, w_in, rnn_conv_w, rnn_conv_b, lru_wa, lru_ba, lru_wx, lru_bx,
              lru_lambda, w_rnn_proj, conf_conv_w, conf_conv_b, conf_norm_g, conf_norm_b, w_conf_proj,
              gate_b, w_out, ln1_g, ln1_b, w_ffn_in, w_ffn_out, ln2_g, ln2_b):
    weights = (w_in, rnn_conv_w, rnn_conv_b, lru_wa, lru_ba, lru_wx, lru_bx, lru_lambda,
               w_rnn_proj, conf_conv_w, conf_conv_b, conf_norm_g, conf_norm_b, w_conf_proj,
               gate_b, w_out, ln1_g, ln1_b, w_ffn_in, w_ffn_out, ln2_g, ln2_b)
    y_prompt = trunk(x_prompt, weights)
    y_sample = trunk(x_sample, weights)
    return (y_prompt, y_sample)
```

```python
import numpy as np
from contextlib import ExitStack
import concourse.bass as bass
import concourse.mybir as mybir
from concourse.bass_utils import run_bass_kernel_spmd

F32 = mybir.dt.float32
BF16 = mybir.dt.bfloat16
AF = mybir.ActivationFunctionType
ALU = mybir.AluOpType

P = 128
D = 1024
NB = 8
T = 512
HL = 16
TC = T + 2 * HL
HC = TC // 2
DFF = 2816
NFB = DFF // P
SEQ = 4096
SAMP = 16384
CH = SAMP // 8
ALPHA = 2.0 ** 0.25
EPS = 1e-5
NT_OWN = 20
NT_S = SAMP // T - CH // T
NT1 = NT_OWN + NT_S
XS_ROWS = 2 * (SEQ + 2 * HL) + CH + 2 * HL
NSLOT = 3
SLOT_ELEMS = 8192

CW4, CB4, BA, BX, LAM, CW31, CB31, CNG, CNB, GB, L1G, L1B = 0, 32, 40, 56, 72, 88, 336, 344, 352, 360, 376, 384
NPP = 392
HBA, HBX, KK, HK, HCB4, K256, AG1, AB1, HCW31 = 0, 16, 32, 48, 64, 72, 88, 96, 104
NDP = 352


class Reg:
    __slots__ = ("w", "rs")

    def __init__(self):
        self.w = {}
        self.rs = {}


class Sem:
    def __init__(self, h):
        self.h = h
        self.n = 0


class Eng:
    def __init__(self, h, sem):
        self.h = h
        self.sem = sem
        self.waited = {}


class Builder:
    def __init__(self, nc, es):
        self.nc = nc
        self.es = es
        self.nsem = 0
        self.PE = Eng(nc.tensor, self.sem("pe"))
        self.ACT = Eng(nc.scalar, self.sem("act"))
        self.DVE = Eng(nc.vector, self.sem("dve"))
        self.POOL = Eng(nc.gpsimd, self.sem("pool"))
        self.SP = Eng(nc.sync, None)

    def sem(self, name):
        self.nsem += 1
        return Sem(self.es.enter_context(self.nc.semaphore("s_%s_%d" % (name, self.nsem))))

    def _wait(self, eng, toks):
        best = {}
        for (s, v) in toks:
            if v > best.get(s, 0):
                best[s] = v
        for s, v in best.items():
            if eng.waited.get(s, 0) >= v:
                continue
            eng.h.wait_ge(s.h, v)
            eng.waited[s] = v

    @staticmethod
    def _deps(reads, writes):
        toks = []
        for r in reads:
            toks.extend(r.w.items())
        for w in writes:
            toks.extend(w.w.items())
            toks.extend(w.rs.items())
        return toks

    @staticmethod
    def _commit(tok, reads, writes):
        s, v = tok
        for r in reads:
            if r.rs.get(s, 0) < v:
                r.rs[s] = v
        for w in writes:
            if w.w.get(s, 0) < v:
                w.w[s] = v
            w.rs = {}

    def op(self, eng, reads, writes, fn):
        self._wait(eng, self._deps(reads, writes))
        ins = fn()
        eng.sem.n += 1
        ins.then_inc(eng.sem.h, 1)
        self._commit((eng.sem, eng.sem.n), reads, writes)

    def group(self, eng, reads, writes, fns):
        self._wait(eng, self._deps(reads, writes))
        ins = None
        for f in fns:
            ins = f()
        eng.sem.n += 1
        ins.then_inc(eng.sem.h, 1)
        self._commit((eng.sem, eng.sem.n), reads, writes)

    def dma(self, eng, dsem, reads, writes, fns):
        self._wait(eng, self._deps(reads, writes))
        for f in fns:
            ins = f()
            dsem.n += 16
            ins.then_inc(dsem.h, 16)
        self._commit((dsem, dsem.n), reads, writes)


def build_program():
    nc = bass.Bass("TRN2", target_bir_lowering=False)
    dt = nc.dram_tensor
    xs = dt("xs", [XS_ROWS, D], F32, kind="ExternalInput").ap()
    xsamp = dt("xsamp", [NT_S * TC, D], F32, kind="ExternalInput").ap()
    w_in = dt("w_in", [D, 6 * D], F32, kind="ExternalInput").ap()
    w_rp = dt("w_rp", [D, D], F32, kind="ExternalInput").ap()
    w_cp = dt("w_cp", [D, D], F32, kind="ExternalInput").ap()
    w_o = dt("w_o", [D, D], F32, kind="ExternalInput").ap()
    w_fi = dt("w_fi", [D, 2 * DFF], F32, kind="ExternalInput").ap()
    w_fo = dt("w_fo", [DFF, D], F32, kind="ExternalInput").ap()
    lruw_d = dt("lruw", [P, 4 * NB * P], F32, kind="ExternalInput").ap()
    pp_d = dt("pp", [P, NPP], F32, kind="ExternalInput").ap()
    lnb_d = dt("lnb", [P, 2, D], F32, kind="ExternalInput").ap()
    msk_d = dt("msk", [P, 4, NT_S], F32, kind="ExternalInput").ap()
    idn_d = dt("idn", [P, P], F32, kind="ExternalInput").ap()
    y = dt("y", [NT_OWN * T, D], F32, kind="ExternalOutput").ap()
    w_in_b = dt("w_in_b", [D, 6 * D], BF16).ap()
    w_rp_b = dt("w_rp_b", [D, D], BF16).ap()
    w_cp_b = dt("w_cp_b", [D, D], BF16).ap()
    w_o_b = dt("w_o_b", [D, D], BF16).ap()
    w_fi_b = dt("w_fi_b", [D, 2 * DFF], BF16).ap()
    w_fo_b = dt("w_fo_b", [DFF, D], BF16).ap()
    dgs = dt("dgs", [NB, P, 31 * P], BF16).ap()

    with ExitStack() as es:
        B = Builder(nc, es)
        PE, ACT, DVE, POOL, SP = B.PE, B.ACT, B.DVE, B.POOL, B.SP

        def sb(name, shape, dtype=F32):
            return es.enter_context(nc.sbuf_tensor("sb_" + name, shape, dtype))

        ident = sb("ident", [P, P]); identb = sb("identb", [P, P], BF16)
        lruw = sb("lruw", [P, 4, NB, P], BF16)
        dg4 = sb("dg4", [P, NB, 4, P], BF16)
        dg31 = sb("dg31", [P, 2, 31, P], BF16)
        pp = sb("pp", [P, NPP]); dp = sb("dp", [P, NDP])
        lnb = sb("lnb", [P, 2, D])
        msk = sb("msk", [P, 4, NT_S])
        SR = sb("SR", [P, NT1, 16]); BBt = sb("BB", [P, NT1, 16]); AA = SR
        CAR = sb("CAR", [P, NT_OWN, 16])
        fold = sb("fold", [P, 4, 16])
        mhalf = sb("mhalf", [P, 1])
        tmp16 = sb("tmp16", [P, 6, 16])
        lst = sb("lst", [P, 4, 2, 6]); lmv = sb("lmv", [P, 4, 2]); lve = sb("lve", [P, 4])
        lrs = sb("lrs", [P, 4]); lnm = sb("lnm", [P, 4])
        wsl = [sb("wslot%d" % i, [P, SLOT_ELEMS], BF16) for i in range(NSLOT)]
        tokbuf = sb("tokbuf", [P, 4, D])
        xh = sb("xh", [32, D])
        xT = sb("xT", [P, NB, TC], BF16)
        xrn = sb("xrn", [P, NB, TC], BF16)
        hg = sb("hg", [P, NB, T], BF16)
        ub = sb("ub", [P, NB, T], BF16)
        Y = sb("Y", [P, NB, T])
        Z = sb("Z", [P, 30, T], BF16)
        ps = es.enter_context(nc.psum_tensor("ps", [P, 8, T], F32))

        R = lambda: Reg()
        r_ident, r_identb, r_lruw, r_dg4, r_pp, r_dp, r_lnb, r_msk = R(), R(), R(), R(), R(), R(), R(), R()
        r_dg31 = [R(), R()]
        r_SR, r_BB, r_CAR, r_fold, r_mhalf, r_tmp16 = R(), R(), R(), R(), R(), R()
        r_AA = r_SR
        r_ln = [R() for _ in range(4)]
        r_wsl = [R() for _ in range(NSLOT)]
        r_tok = [R() for _ in range(4)]
        r_xh = R()
        r_xT = [R() for _ in range(NB)]
        r_xrn = [R() for _ in range(NB)]
        r_hg = [R() for _ in range(NB)]
        r_ub = [R() for _ in range(NB)]
        r_Y = [R() for _ in range(NB)]
        r_Z = [R() for _ in range(30)]
        r_ps = [R() for _ in range(8)]
        r_win = [R() for _ in range(6)]
        r_wrp, r_wcp, r_wo, r_wfi, r_wfo = R(), R(), R(), R(), R()
        r_y = R()

        d_const = B.sem("dc")
        d_ws = [B.sem("dws") for _ in range(NSLOT)]
        d_tl = [B.sem("dtl") for _ in range(4)]
        d_to = [B.sem("dto") for _ in range(4)]
        d_xh = B.sem("dxh")
        d_ty = [B.sem("dty") for _ in range(4)]
        d_dgw = [B.sem("ddgw"), B.sem("ddgw")]
        d_dgr = [B.sem("ddgr"), B.sem("ddgr")]
        r_dgs = [R() for _ in range(NB)]
        d_ci = [B.sem("dci"), B.sem("dci")]
        d_co = [B.sem("dco"), B.sem("dco")]

        def zf(u, n=1):
            return Z[:, u:u + 2 * n, :].rearrange("p a b -> p (a b)").bitcast(F32), r_Z[u:u + 2 * n]

        def zb(u, n=1):
            return Z[:, u:u + n, :].rearrange("p a b -> p (a b)"), r_Z[u:u + n]

        psc = [0]

        ps_split = [False]
        gsc = [0]

        def bank():
            if ps_split[0]:
                i = psc[0] % 4
            else:
                i = psc[0] % 8
            psc[0] += 1
            return ps[:, i, :], [r_ps[i]]

        def gbank():
            if not ps_split[0]:
                return bank()
            i = 4 + gsc[0] % 4
            gsc[0] += 1
            return ps[:, i, :], [r_ps[i]]

        def pair():
            if psc[0] % 2:
                psc[0] += 1
            i = psc[0] % 8
            psc[0] += 2
            return ps[:, i:i + 2, :].rearrange("p a b -> p (a b)"), [r_ps[i], r_ps[i + 1]]

        evc = [0]
        evc_dve_only = [False]
        evc_force = [None]

        def evac(out, in_, reads, writes):
            evc[0] += 1
            if evc_force[0] == "act" or (evc_force[0] is None and evc[0] % 2 and not evc_dve_only[0]):
                B.op(ACT, reads, writes, lambda: nc.scalar.copy(out=out, in_=in_))
            else:
                B.op(DVE, reads, writes, lambda: nc.vector.tensor_copy(out=out, in_=in_))

        def act(out, in_, func, reads, writes, scale=1.0, bias=0.0):
            B.op(ACT, reads, writes,
                 lambda: nc.scalar.activation(out=out, in_=in_, func=func, bias=bias, scale=scale))

        def col(t, c):
            return t[:, c:c + 1]

        B.dma(SP, d_const, [], [r_ident], [lambda: nc.sync.dma_start(out=ident[:], in_=idn_d)])
        B.dma(SP, B.sem("dc2"), [], [r_pp], [lambda: nc.sync.dma_start(out=pp[:], in_=pp_d)])
        B.dma(SP, B.sem("dc3"), [], [r_lnb], [lambda: nc.sync.dma_start(out=lnb[:], in_=lnb_d)])
        B.dma(SP, B.sem("dc4"), [], [r_msk], [lambda: nc.sync.dma_start(out=msk[:], in_=msk_d)])
        B.op(DVE, [r_ident], [r_identb], lambda: nc.vector.tensor_copy(out=identb[:], in_=ident[:]))
        B.op(DVE, [], [r_mhalf], lambda: nc.vector.memset(mhalf[:], -0.5))
        B.op(DVE, [], [r_CAR], lambda: nc.vector.memset(CAR[:], 0.0))
        B.op(DVE, [], [r_SR], lambda: nc.vector.memset(SR[:], 0.0))

        def ts(out, in0, s1, op0, s2=None, op1=None, reads=(), writes=()):
            if op1 is None:
                B.op(DVE, list(reads), list(writes),
                     lambda: nc.vector.tensor_scalar(out=out, in0=in0, scalar1=s1, scalar2=None, op0=op0))
            else:
                B.op(DVE, list(reads), list(writes),
                     lambda: nc.vector.tensor_scalar(out=out, in0=in0, scalar1=s1, scalar2=s2, op0=op0, op1=op1))

        ts(dp[:, HBA:HBA + 16], pp[:, BA:BA + 16], 0.5, ALU.mult, reads=[r_pp], writes=[r_dp])
        ts(dp[:, HBX:HBX + 16], pp[:, BX:BX + 16], 0.5, ALU.mult, reads=[r_pp], writes=[r_dp])
        ts(dp[:, HCB4:HCB4 + 8], pp[:, CB4:CB4 + 8], 0.5, ALU.mult, reads=[r_pp], writes=[r_dp])
        ts(dp[:, HCW31:HCW31 + 248], pp[:, CW31:CW31 + 248], 0.5, ALU.mult, reads=[r_pp], writes=[r_dp])
        ts(dp[:, AG1:AG1 + 8], pp[:, L1G:L1G + 8], ALPHA, ALU.mult, reads=[r_pp], writes=[r_dp])
        ts(dp[:, AB1:AB1 + 8], pp[:, L1B:L1B + 8], ALPHA, ALU.mult, reads=[r_pp], writes=[r_dp])
        t_x, t_ax, t_e, t_l, t_m, t_sp = (tmp16[:, i, :] for i in range(6))
        ts(t_x, pp[:, LAM:LAM + 16], -1.0, ALU.mult, reads=[r_pp], writes=[r_tmp16])
        B.op(DVE, [r_tmp16, r_pp], [r_tmp16], lambda: nc.vector.tensor_tensor(
            out=t_ax, in0=t_x, in1=pp[:, LAM:LAM + 16], op=ALU.max))
        act(t_e, t_ax, AF.Exp, [r_tmp16], [r_tmp16], scale=-1.0)
        act(t_l, t_e, AF.Ln, [r_tmp16], [r_tmp16], scale=1.0, bias=1.0)
        ts(t_m, t_x, 0.0, ALU.max, reads=[r_tmp16], writes=[r_tmp16])
        B.op(DVE, [r_tmp16], [r_tmp16], lambda: nc.vector.tensor_tensor(out=t_sp, in0=t_m, in1=t_l, op=ALU.add))
        ts(dp[:, KK:KK + 16], t_sp, -8.0, ALU.mult, reads=[r_tmp16], writes=[r_dp])
        ts(dp[:, HK:HK + 16], t_sp, -4.0, ALU.mult, reads=[r_tmp16], writes=[r_dp])
        ts(dp[:, K256:K256 + 16], t_sp, -8.0 * (T / 2), ALU.mult, reads=[r_tmp16], writes=[r_dp])
        B.op(POOL, [r_identb, r_pp], [r_dg4], lambda: nc.gpsimd.tensor_tensor(
            out=dg4[:].rearrange("p a k q -> p (a k) q"),
            in0=identb[:].unsqueeze(1).to_broadcast([P, NB * 4, P]),
            in1=pp[:, CW4:CW4 + 32].unsqueeze(2).to_broadcast([P, NB * 4, P]), op=ALU.mult))

        castin = Y[:].rearrange("p (a b) c -> p a (b c)", a=2)
        castout = ub[:].rearrange("p (a b) c -> p a (b c)", a=2)
        r_cin = [r_Y[0:4], r_Y[4:8]]
        r_cout = [r_ub[0:4], r_ub[4:8]]
        B.dma(POOL, d_ci[0], [], r_cin[0] + r_cin[1],
              [lambda: nc.gpsimd.dma_start(out=Y[:].rearrange("p a b -> p (a b)"), in_=lruw_d)])
        B.op(POOL, r_cin[0] + r_cin[1], [r_lruw], lambda: nc.gpsimd.tensor_copy(
            out=lruw[:].rearrange("p a b c -> p (a b c)"), in_=Y[:].rearrange("p a b -> p (a b)")))
        chunks = []
        for cg in range(6):
            for kc in range(NB):
                chunks.append((w_in, w_in_b, kc, cg * D, D, r_win[cg]))
        for (src, dst, reg) in ((w_rp, w_rp_b, r_wrp), (w_cp, w_cp_b, r_wcp), (w_o, w_o_b, r_wo)):
            for kc in range(NB):
                chunks.append((src, dst, kc, 0, D, reg))
        for kc in range(NB):
            for (c0, n) in ((0, 2048), (2048, 2048), (4096, 1536)):
                chunks.append((w_fi, w_fi_b, kc, c0, n, r_wfi))
        for f in range(NFB):
            chunks.append((w_fo, w_fo_b, f, 0, D, r_wfo))

        def cast_load(i):
            src, dst, kc, c0, n, reg = chunks[i]
            b = i % 2
            B.dma(POOL, d_ci[b], [], r_cin[b], [lambda: nc.gpsimd.dma_start(
                out=castin[:, b, 0:n], in_=src[kc * P:(kc + 1) * P, c0:c0 + n])])

        def cast_do(i):
            src, dst, kc, c0, n, reg = chunks[i]
            b = i % 2
            B.op(POOL, r_cin[b], r_cout[b],
                 lambda: nc.gpsimd.tensor_copy(out=castout[:, b, 0:n], in_=castin[:, b, 0:n]))
            B.dma(POOL, d_co[b], r_cout[b], [reg], [lambda: nc.gpsimd.dma_start(
                out=dst[kc * P:(kc + 1) * P, c0:c0 + n], in_=castout[:, b, 0:n])])

        cast_load(0)
        cast_load(1)
        cast_pos = [0]

        def cast_steps(n):
            for _ in range(n):
                i = cast_pos[0]
                if i >= len(chunks):
                    return
                cast_do(i)
                if i + 2 < len(chunks):
                    cast_load(i + 2)
                cast_pos[0] += 1

        cast_steps(16)
        for j in range(NB):
            sj = j % 2
            B.op(POOL, [r_identb, r_dp], [r_dg31[sj]], lambda sj=sj, j=j: nc.gpsimd.tensor_tensor(
                out=dg31[:, sj], in0=identb[:].unsqueeze(1).to_broadcast([P, 31, P]),
                in1=dp[:, HCW31 + j * 31:HCW31 + (j + 1) * 31].unsqueeze(2).to_broadcast([P, 31, P]),
                op=ALU.mult))
            B.dma(POOL, d_dgw[sj], [r_dg31[sj]], [r_dgs[j]], [lambda sj=sj, j=j: nc.gpsimd.dma_start(
                out=dgs[j], in_=dg31[:, sj].rearrange("p k q -> p (k q)"))])

        plan = [("win", 0), ("win", 0)]
        for t_ in range(NT_OWN):
            plan += [("win", 1), ("win", 3), ("win", 2)]
            plan += [("mix", g) for g in range(4)]
            plan += [("wout", 0)]
            if t_ + 1 < NT_OWN:
                plan += [("win", 0)]
            plan += [("ffi", g) for g in range(6)]
            plan += [("ffo", g) for g in range(4)]
        wst = {"issued": 0, "next": 0}

        def ws_issue(i):
            kind, g = plan[i]
            s = i % NSLOT
            W = wsl[s]
            if kind == "win":
                fns = [lambda: nc.sync.dma_start(
                    out=W[:].rearrange("p (k c) -> p k c", c=D),
                    in_=w_in_b.rearrange("(k p) c -> p k c", p=P)[:, :, g * D:(g + 1) * D])]
                srcs = [r_win[g]]
            elif kind == "wout":
                fns = [lambda: nc.sync.dma_start(
                    out=W[:].rearrange("p (k c) -> p k c", c=D),
                    in_=w_o_b.rearrange("(k p) c -> p k c", p=P))]
                srcs = [r_wo]
            elif kind == "mix":
                Wv = W[:].rearrange("p (m k c) -> p m k c", m=4, k=NB)
                c0 = g * 256
                fns = [
                    lambda: nc.sync.dma_start(out=Wv[:, 0], in_=w_rp_b.rearrange("(k p) c -> p k c", p=P)[:, :, c0:c0 + 256]),
                    lambda: nc.sync.dma_start(out=Wv[:, 1], in_=w_cp_b.rearrange("(k p) c -> p k c", p=P)[:, :, c0:c0 + 256]),
                    lambda: nc.sync.dma_start(out=Wv[:, 2], in_=w_in_b.rearrange("(k p) c -> p k c", p=P)[:, :, 4 * D + c0:4 * D + c0 + 256]),
                    lambda: nc.sync.dma_start(out=Wv[:, 3], in_=w_in_b.rearrange("(k p) c -> p k c", p=P)[:, :, 5 * D + c0:5 * D + c0 + 256]),
                ]
                srcs = [r_wrp, r_wcp, r_win[4], r_win[5]]
            elif kind == "ffi":
                Wv = W[:].rearrange("p (m k c) -> p m k c", m=2, k=NB)
                c0 = g * 512
                n = min(512, DFF - c0)
                fns = [
                    lambda: nc.sync.dma_start(out=Wv[:, 0, :, 0:n], in_=w_fi_b.rearrange("(k p) c -> p k c", p=P)[:, :, c0:c0 + n]),
                    lambda: nc.sync.dma_start(out=Wv[:, 1, :, 0:n], in_=w_fi_b.rearrange("(k p) c -> p k c", p=P)[:, :, DFF + c0:DFF + c0 + n]),
                ]
                srcs = [r_wfi]
            else:
                Wv = W[:, 0:NFB * 256].rearrange("p (f c) -> p f c", c=256)
                fns = [lambda: nc.sync.dma_start(
                    out=Wv, in_=w_fo_b.rearrange("(f p) c -> p f c", p=P)[:, :, g * 256:(g + 1) * 256])]
                srcs = [r_wfo]
            B.dma(SP, d_ws[s], srcs, [r_wsl[s]], fns)

        def ws_acquire(kind, g, hold=0):
            n = wst["next"]
            assert plan[n] == (kind, g), (plan[n], kind, g)
            wst["next"] += 1
            while wst["issued"] < min(len(plan), n + NSLOT - hold):
                ws_issue(wst["issued"])
                wst["issued"] += 1
            s = n % NSLOT
            return wsl[s], [r_wsl[s]]

        def load_tok(src, row0):
            for s in range(4):
                B.dma(SP, d_tl[s], [], [r_tok[s]], [lambda s=s: nc.sync.dma_start(
                    out=tokbuf[:, s, :], in_=src[row0 + s * P:row0 + (s + 1) * P, :])])

        Ystage = Y[:].rearrange("p (s b) c -> p s (b c)", s=4)

        def stage1(src, ext0, use_y=False):
            if use_y:
                xst, r_xst = Ystage, r_Y
                for s in range(4):
                    B.dma(SP, d_ty[s], [], r_Y[2 * s:2 * s + 2], [lambda s=s: nc.sync.dma_start(
                        out=Ystage[:, s, :], in_=src[ext0 + HL + s * P:ext0 + HL + (s + 1) * P, :])])
            else:
                xst, r_xst = tokbuf, r_tok
                load_tok(src, ext0 + HL)
            B.dma(SP, d_xh, [], [r_xh], [
                lambda: nc.sync.dma_start(out=xh[0:HL, :], in_=src[ext0:ext0 + HL, :]),
                lambda: nc.sync.dma_start(out=xh[HL:2 * HL, :], in_=src[ext0 + HL + T:ext0 + TC, :])])
            for c in range(NB):
                bk, rb = bank()
                B.group(PE, r_xst + [r_ident], rb, [
                    (lambda s=s, c=c, bk=bk: nc.tensor.transpose(
                        out=bk[:, s * P:(s + 1) * P], in_=xst[:, s, c * P:(c + 1) * P], identity=ident[:]))
                    for s in range(4)])
                evac(xT[:, c, HL:HL + T], bk, rb, [r_xT[c]])
            bk, rb = bank()
            B.group(PE, [r_xh, r_ident], rb, [
                (lambda c=c, bk=bk: nc.tensor.transpose(
                    out=bk[:, c * 32:(c + 1) * 32], in_=xh[0:32, c * P:(c + 1) * P], identity=ident[0:32, 0:32]))
                for c in range(NB)])
            hv = bk[:, 0:256].rearrange("p (a b) -> p a b", b=32)
            evac(xT[:, :, 0:HL], hv[:, :, 0:HL], rb, r_xT)
            evac(xT[:, :, HL + T:TC], hv[:, :, HL:2 * HL], rb, r_xT)

        def stage2a(W, rW):
            Wv = W[:].rearrange("p (k c) -> p k c", c=D)
            for j in range(NB):
                for h in range(2):
                    bk, rb = bank()
                    B.group(PE, r_xT + rW, rb, [
                        (lambda kc=kc, j=j, h=h, bk=bk: nc.tensor.matmul(
                            bk[:, 0:HC], lhsT=Wv[:, kc, j * P:(j + 1) * P], rhs=xT[:, kc, h * HC:(h + 1) * HC],
                            start=(kc == 0), stop=(kc == NB - 1)))
                        for kc in range(NB)])
                    evac(xrn[:, j, h * HC:(h + 1) * HC], bk[:, 0:HC], rb, [r_xrn[j]])

        def stage2b(j, g1, pass2, car_g, mid=None):
            s = j % 2
            base = 11 * s
            a_d = [zf(base + 0), zf(base + 4)]
            t_d = [zf(base + 2), zf(base + 6)]
            xch, r_xch = zf(base + 8)
            xcb, r_xcb = zb(base + 10)
            bc, rbc = bank()
            B.group(PE, [r_xrn[j], r_dg4], rbc, [
                (lambda k=k: nc.tensor.matmul(bc, lhsT=dg4[:, j, k, :], rhs=xrn[:, j, HL - 2 + k:HL - 2 + k + T],
                                              start=(k == 0), stop=(k == 3)))
                for k in range(4)])
            B.op(DVE, rbc + [r_pp], r_xcb, lambda: nc.vector.tensor_scalar(
                out=xcb, in0=bc, scalar1=col(pp, CB4 + j), scalar2=None, op0=ALU.add))
            B.op(DVE, rbc + [r_dp], r_xch, lambda: nc.vector.tensor_scalar(
                out=xch, in0=bc, scalar1=0.5, scalar2=col(dp, HCB4 + j), op0=ALU.mult, op1=ALU.add))
            if mid is not None:
                mid()
            banks = []
            for d in range(2):
                br, rbr = gbank()
                bi, rbi = gbank()
                B.group(PE, r_xcb + [r_lruw], rbr, [lambda br=br, d=d: nc.tensor.matmul(
                    br, lhsT=lruw[:, 2 * d, j, :], rhs=xcb, start=True, stop=True)])
                B.group(PE, r_xcb + [r_lruw], rbi, [lambda bi=bi, d=d: nc.tensor.matmul(
                    bi, lhsT=lruw[:, 2 * d + 1, j, :], rhs=xcb, start=True, stop=True)])
                banks.append((br, rbr, bi, rbi))
            for d in range(2):
                br, rbr, bi, rbi = banks[d]
                cdx = d * 8 + j
                if pass2:
                    act(br, br, AF.Tanh, rbr + [r_dp], rbr, scale=0.5, bias=col(dp, HBA + cdx))
                else:
                    B.op(ACT, rbr + [r_dp], rbr + [r_SR], lambda br=br, cdx=cdx: nc.scalar.activation(
                        out=br, in_=br, func=AF.Tanh, bias=col(dp, HBA + cdx), scale=0.5,
                        accum_out=SR[:, g1, cdx:cdx + 1]))
                act(bi, bi, AF.Tanh, rbi + [r_dp], rbi, scale=0.5, bias=col(dp, HBX + cdx))
                act(a_d[d][0], br, AF.Exp, rbr + [r_dp], a_d[d][1], scale=col(dp, HK + cdx), bias=col(dp, HK + cdx))
                act(br, br, AF.Exp, rbr + [r_dp], rbr, scale=col(dp, KK + cdx), bias=col(dp, KK + cdx))
                B.op(DVE, rbr, rbr, lambda br=br: nc.vector.tensor_scalar(
                    out=br, in0=br, scalar1=0.99999994, scalar2=None, op0=ALU.min))
            for d in range(2):
                br, rbr, bi, rbi = banks[d]
                act(br, br, AF.Sqrt, rbr, rbr, scale=-1.0, bias=1.0)
            for d in range(2):
                br, rbr, bi, rbi = banks[d]
                cdx = d * 8 + j
                tt, rtt = t_d[d]
                aa, raa = a_d[d]
                B.op(DVE, rbi + r_xch, rtt, lambda tt=tt, bi=bi: nc.vector.scalar_tensor_tensor(
                    out=tt, in0=bi, scalar=1.0, in1=xch, op0=ALU.add, op1=ALU.mult))
                B.op(DVE, rtt + rbr, rtt, lambda tt=tt, br=br: nc.vector.tensor_tensor(
                    out=tt, in0=tt, in1=br, op=ALU.mult))
                init = CAR[:, car_g, cdx:cdx + 1] if pass2 else 0.0
                rinit = [r_CAR] if pass2 else []
                if d == 0:
                    B.op(DVE, rtt + raa + rinit, rtt, lambda tt=tt, aa=aa, init=init: nc.vector.tensor_tensor_scan(
                        out=tt, data0=aa, data1=tt, initial=init, op0=ALU.mult, op1=ALU.add))
                else:
                    B.op(DVE, rtt + raa + rinit, rtt, lambda tt=tt, aa=aa, init=init: nc.vector.tensor_tensor_scan(
                        out=tt[:, ::-1], data0=aa[:, ::-1], data1=tt[:, ::-1], initial=init,
                        op0=ALU.mult, op1=ALU.add))
                if not pass2:
                    endc = T - 1 if d == 0 else 0
                    B.op(DVE, rtt, [r_BB], lambda tt=tt, endc=endc, cdx=cdx: nc.vector.tensor_copy(
                        out=BBt[:, g1, cdx:cdx + 1], in_=tt[:, endc:endc + 1]))
            if pass2:
                tf, rtf = t_d[0]
                tb, rtb = t_d[1]
                B.op(POOL, rtf + rtb, rtf, lambda: nc.gpsimd.tensor_tensor(out=tf, in0=tf, in1=tb, op=ALU.add))
                B.op(POOL, rtf + [r_Y[j]], [r_hg[j]], lambda: nc.gpsimd.tensor_tensor(
                    out=hg[:, j, :], in0=tf, in1=Y[:, j, :], op=ALU.mult))


        def p1_views(j):
            s = j % 2
            base = 11 * s
            return dict(a=[zf(base + 0), zf(base + 4)], t=[zf(base + 2), zf(base + 6)], xch=zf(base + 8),
                        xcb=zb(base + 10), q=[zf(22 + 4 * s), zf(24 + 4 * s)])

        p1_banks = {}

        def p1_A(j, g1):
            v = p1_views(j)
            xch, r_xch = v["xch"]
            xcb, r_xcb = v["xcb"]
            bc, rbc = bank()
            B.group(PE, [r_xrn[j], r_dg4], rbc, [
                (lambda k=k: nc.tensor.matmul(bc, lhsT=dg4[:, j, k, :], rhs=xrn[:, j, HL - 2 + k:HL - 2 + k + T],
                                              start=(k == 0), stop=(k == 3)))
                for k in range(4)])
            B.op(DVE, rbc + [r_pp], r_xcb, lambda: nc.vector.tensor_scalar(
                out=xcb, in0=bc, scalar1=col(pp, CB4 + j), scalar2=None, op0=ALU.add))
            B.op(DVE, rbc + [r_dp], r_xch, lambda: nc.vector.tensor_scalar(
                out=xch, in0=bc, scalar1=0.5, scalar2=col(dp, HCB4 + j), op0=ALU.mult, op1=ALU.add))
            bl = []
            for d in range(2):
                br, rbr = bank()
                bi, rbi = bank()
                B.group(PE, r_xcb + [r_lruw], rbr, [lambda br=br, d=d: nc.tensor.matmul(
                    br, lhsT=lruw[:, 2 * d, j, :], rhs=xcb, start=True, stop=True)])
                B.group(PE, r_xcb + [r_lruw], rbi, [lambda bi=bi, d=d: nc.tensor.matmul(
                    bi, lhsT=lruw[:, 2 * d + 1, j, :], rhs=xcb, start=True, stop=True)])
                bl.append((br, rbr, bi, rbi))
            p1_banks[j] = bl

        def p1_B(j, g1):
            v = p1_views(j)
            xch, r_xch = v["xch"]
            for d in range(2):
                br, rbr, bi, rbi = p1_banks[j][d]
                cdx = d * 8 + j
                aa, raa = v["a"][d]
                qq, rqq = v["q"][d]
                tt, rtt = v["t"][d]
                B.op(ACT, rbr + [r_dp], rbr + [r_SR], lambda br=br, cdx=cdx: nc.scalar.activation(
                    out=br, in_=br, func=AF.Tanh, bias=col(dp, HBA + cdx), scale=0.5,
                    accum_out=SR[:, g1, cdx:cdx + 1]))
                act(aa, br, AF.Exp, rbr + [r_dp], raa, scale=col(dp, HK + cdx), bias=col(dp, HK + cdx))
                act(qq, br, AF.Exp, rbr + [r_dp], rqq, scale=col(dp, KK + cdx), bias=col(dp, KK + cdx))
                act(bi, bi, AF.Tanh, rbi + [r_dp], rbi, scale=0.5, bias=col(dp, HBX + cdx))
                B.op(DVE, rbi + r_xch, rtt, lambda tt=tt, bi=bi: nc.vector.scalar_tensor_tensor(
                    out=tt, in0=bi, scalar=1.0, in1=xch, op0=ALU.add, op1=ALU.mult))
                B.op(DVE, rqq, rqq, lambda qq=qq: nc.vector.tensor_scalar(
                    out=qq, in0=qq, scalar1=0.99999994, scalar2=None, op0=ALU.min))

        def p1_S(j):
            v = p1_views(j)
            for d in range(2):
                qq, rqq = v["q"][d]
                act(qq, qq, AF.Sqrt, rqq, rqq, scale=-1.0, bias=1.0)

        def p1_K(j, g1):
            v = p1_views(j)
            for d in range(2):
                cdx = d * 8 + j
                aa, raa = v["a"][d]
                qq, rqq = v["q"][d]
                tt, rtt = v["t"][d]
                B.op(DVE, rtt + rqq, rtt, lambda tt=tt, qq=qq: nc.vector.tensor_tensor(
                    out=tt, in0=tt, in1=qq, op=ALU.mult))
                if d == 0:
                    B.op(DVE, rtt + raa, rtt, lambda tt=tt, aa=aa: nc.vector.tensor_tensor_scan(
                        out=tt, data0=aa, data1=tt, initial=0.0, op0=ALU.mult, op1=ALU.add))
                else:
                    B.op(DVE, rtt + raa, rtt, lambda tt=tt, aa=aa: nc.vector.tensor_tensor_scan(
                        out=tt[:, ::-1], data0=aa[:, ::-1], data1=tt[:, ::-1], initial=0.0,
                        op0=ALU.mult, op1=ALU.add))
                endc = T - 1 if d == 0 else 0
                B.op(POOL, rtt, [r_BB], lambda tt=tt, endc=endc, cdx=cdx: nc.gpsimd.tensor_copy(
                    out=BBt[:, g1, cdx:cdx + 1], in_=tt[:, endc:endc + 1]))

        def p1_blocks(g1, W, rW, next_tile):
            p1_A(0, g1); p1_B(0, g1); p1_A(1, g1); p1_B(1, g1); p1_A(2, g1)
            for jp in range(0, NB, 2):
                p1_S(jp); p1_S(jp + 1)
                p1_K(jp, g1); p1_K(jp + 1, g1)
                if jp + 2 < NB:
                    p1_B(jp + 2, g1); p1_A(jp + 3, g1); p1_B(jp + 3, g1)
                    if next_tile is not None and jp == 0:
                        stage1(*next_tile)
                    if next_tile is not None and jp + 3 == NB - 1:
                        evc_force[0] = "act"
                        stage2a(W, rW)
                        evc_force[0] = None
                if jp + 4 < NB:
                    p1_A(jp + 4, g1)

        def ln_stats(s, src, rsrc):
            B.op(DVE, rsrc, [r_ln[s]], lambda: nc.vector.bn_stats(out=lst[:, s, 0, :], in_=src[:, 0:512]))
            B.op(DVE, rsrc, [r_ln[s]], lambda: nc.vector.bn_stats(out=lst[:, s, 1, :], in_=src[:, 512:1024]))
            B.op(DVE, [r_ln[s]], [r_ln[s]], lambda: nc.vector.bn_aggr(
                out=lmv[:, s, :], in_=lst[:, s].rearrange("p a b -> p (a b)")))
            B.op(DVE, [r_ln[s]], [r_ln[s]], lambda: nc.vector.tensor_scalar(
                out=lve[:, s:s + 1], in0=lmv[:, s, 1:2], scalar1=EPS, scalar2=None, op0=ALU.add))
            B.op(POOL, [r_ln[s], r_mhalf], [r_ln[s]], lambda: nc.gpsimd.tensor_tensor(
                out=lrs[:, s:s + 1], in0=lve[:, s:s + 1], in1=mhalf[:], op=ALU.pow))
            B.op(DVE, [r_ln[s]], [r_ln[s]], lambda: nc.vector.scalar_tensor_tensor(
                out=lnm[:, s:s + 1], in0=lmv[:, s, 0:1], scalar=-1.0, in1=lrs[:, s:s + 1],
                op0=ALU.mult, op1=ALU.mult))

        W0, rW0 = ws_acquire("win", 0)
        tiles1 = []
        for g in range(NT_OWN):
            seg, i = (g // 8, g % 8) if g < 16 else (2, g - 16)
            tiles1.append((xs, seg * (SEQ + 2 * HL) + i * T))
        for u in range(NT_S):
            tiles1.append((xsamp, u * TC))
        stage1(*tiles1[0])
        stage2a(W0, rW0)
        for g1 in range(NT1):
            p1_blocks(g1, W0, rW0, tiles1[g1 + 1] if g1 + 1 < NT1 else None)
            cast_steps(3)

        cast_steps(len(chunks))
        SRf = SR[:]
        B.op(DVE, [r_SR, r_dp], [r_AA], lambda: nc.vector.tensor_tensor(
            out=AA[:], in0=SRf, in1=dp[:, HK:HK + 16].unsqueeze(1).to_broadcast([P, NT1, 16]), op=ALU.mult))
        B.op(DVE, [r_AA, r_dp], [r_AA], lambda: nc.vector.tensor_tensor(
            out=AA[:], in0=AA[:], in1=dp[:, K256:K256 + 16].unsqueeze(1).to_broadcast([P, NT1, 16]), op=ALU.add))
        act(AA[:], AA[:], AF.Exp, [r_AA], [r_AA])
        for d in range(2):
            sl = slice(d * 8, d * 8 + 8)
            mk = msk[:, 2 * d, :].unsqueeze(2).to_broadcast([P, NT_S, 8])
            omk = msk[:, 2 * d + 1, :].unsqueeze(2).to_broadcast([P, NT_S, 8])
            As = AA[:, NT_OWN:NT1, sl]
            Bs = BBt[:, NT_OWN:NT1, sl]
            B.op(DVE, [r_AA, r_msk], [r_AA], lambda As=As, mk=mk: nc.vector.tensor_tensor(out=As, in0=As, in1=mk, op=ALU.mult))
            B.op(DVE, [r_AA, r_msk], [r_AA], lambda As=As, omk=omk: nc.vector.tensor_tensor(out=As, in0=As, in1=omk, op=ALU.add))
            B.op(DVE, [r_BB, r_msk], [r_BB], lambda Bs=Bs, mk=mk: nc.vector.tensor_tensor(out=Bs, in0=Bs, in1=mk, op=ALU.mult))
        HFc = fold[:, 0, 0:8]
        HBc = fold[:, 0, 8:16]
        tmpc = fold[:, 1, 0:8]
        B.op(DVE, [], [r_fold], lambda: nc.vector.memset(fold[:], 0.0))

        def step(Hout, Hin, g, sl, rH_in, rH_out):
            B.op(DVE, [r_AA] + rH_in, [r_fold], lambda: nc.vector.tensor_tensor(
                out=tmpc, in0=AA[:, g, sl], in1=Hin, op=ALU.mult))
            B.op(DVE, [r_fold, r_BB], rH_out, lambda: nc.vector.tensor_tensor(
                out=Hout, in0=tmpc, in1=BBt[:, g, sl], op=ALU.add))

        fs, bsl = slice(0, 8), slice(8, 16)
        for u in range(NT_S):
            step(HFc, HFc, NT_OWN + u, fs, [r_fold], [r_fold])
        for u in range(NT_S - 1, -1, -1):
            step(HBc, HBc, NT_OWN + u, bsl, [r_fold], [r_fold])
        B.op(DVE, [r_fold], [r_CAR], lambda: nc.vector.tensor_copy(out=CAR[:, 16, 0:8], in_=HFc))
        B.op(DVE, [r_fold], [r_CAR], lambda: nc.vector.tensor_copy(out=CAR[:, 19, 8:16], in_=HBc))
        for (g0, n) in ((0, 8), (8, 8), (16, 4)):
            for g in range(g0, g0 + n - 1):
                step(CAR[:, g + 1, 0:8], CAR[:, g, 0:8], g, fs, [r_CAR], [r_CAR])
            for g in range(g0 + n - 1, g0, -1):
                step(CAR[:, g - 1, 8:16], CAR[:, g, 8:16], g, bsl, [r_CAR], [r_CAR])

        evc_dve_only[0] = False
        mg = xrn
        r_mg = r_xrn
        x1T = hg
        r_x1T = r_hg
        stage1(*tiles1[0])
        W, rW = ws_acquire("win", 0)
        stage2a(W, rW)
        for g in range(NT_OWN):
            src, ext0 = tiles1[g]
            out0 = g * T
            W, rW = ws_acquire("win", 1)
            Wv = W[:].rearrange("p (k c) -> p k c", c=D)
            for j in range(NB):
                bk, rb = bank()
                B.group(PE, r_xT + rW, rb, [
                    (lambda kc=kc, j=j, bk=bk, Wv=Wv: nc.tensor.matmul(
                        bk, lhsT=Wv[:, kc, j * P:(j + 1) * P], rhs=xT[:, kc, HL:HL + T],
                        start=(kc == 0), stop=(kc == NB - 1)))
                    for kc in range(NB)])
                act(Y[:, j, :], bk, AF.Gelu_apprx_tanh, rb, [r_Y[j]])
            Wc, rWc = ws_acquire("win", 3)
            Wu, rWu = ws_acquire("win", 2, hold=1)
            Wcv = Wc[:].rearrange("p (k c) -> p k c", c=D)
            Wuv = Wu[:].rearrange("p (k c) -> p k c", c=D)
            def s3_diag(j):
                s = j % 2
                B.dma(SP, d_dgr[s], [r_dgs[j]], [r_dg31[s]], [lambda s=s, j=j: nc.sync.dma_start(
                    out=dg31[:, s].rearrange("p k q -> p (k q)"), in_=dgs[j])])

            def s3_proj(j):
                s = j % 2
                sig, r_sig = zf(22 + 2 * s)
                uflat, r_u = zb(26 + 2 * s, 2)
                for h in range(2):
                    bv, rbv = bank()
                    bu, rbu = bank()
                    B.group(PE, r_xT + rWc, rbv, [
                        (lambda kc=kc, bv=bv, h=h, j=j: nc.tensor.matmul(
                            bv[:, 0:HC], lhsT=Wcv[:, kc, j * P:(j + 1) * P], rhs=xT[:, kc, h * HC:(h + 1) * HC],
                            start=(kc == 0), stop=(kc == NB - 1)))
                        for kc in range(NB)])
                    B.group(PE, r_xT + rWu, rbu, [
                        (lambda kc=kc, bu=bu, h=h, j=j: nc.tensor.matmul(
                            bu[:, 0:HC], lhsT=Wuv[:, kc, j * P:(j + 1) * P], rhs=xT[:, kc, h * HC:(h + 1) * HC],
                            start=(kc == 0), stop=(kc == NB - 1)))
                        for kc in range(NB)])
                    act(sig[:, 0:HC], bv[:, 0:HC], AF.Tanh, rbv, r_sig, scale=0.5)
                    B.op(DVE, r_sig + rbu, r_u, lambda h=h, bu=bu, sig=sig, uflat=uflat: nc.vector.scalar_tensor_tensor(
                        out=uflat[:, h * HC:(h + 1) * HC], in0=sig[:, 0:HC], scalar=1.0, in1=bu[:, 0:HC],
                        op0=ALU.add, op1=ALU.mult))

            def s3_conv(j):
                s = j % 2
                uflat, r_u = zb(26 + 2 * s, 2)
                bc, rbc = bank()
                B.group(PE, r_u + [r_dg31[s]], rbc, [
                    (lambda k=k, bc=bc, s=s, uflat=uflat: nc.tensor.matmul(
                        bc, lhsT=dg31[:, s, k, :], rhs=uflat[:, 1 + k:1 + k + T], start=(k == 0), stop=(k == 30)))
                    for k in range(31)])
                act(Y[:, j, :], bc, AF.Identity, rbc + [r_pp], [r_Y[j]], scale=1.0, bias=col(pp, CB31 + j))

            ps_split[0] = True
            for j in range(NB):
                s3_diag(j)
                stage2b(j, None, True, g, mid=lambda j=j: s3_proj(j))
                if j >= 1:
                    s3_conv(j - 1)
            s3_conv(NB - 1)
            ps_split[0] = False
            for s in range(4):
                pr, rpr = pair()
                B.group(PE, r_Y + [r_ident], rpr, [
                    (lambda j=j, s=s, pr=pr: nc.tensor.transpose(
                        out=pr[:, j * P:(j + 1) * P], in_=Y[:, j, s * P:(s + 1) * P], identity=ident[:]))
                    for j in range(NB)])
                ln_stats(s, pr, rpr)
                act(tokbuf[:, s, :], pr, AF.Identity, rpr + [r_ln[s]], [r_tok[s]],
                    scale=lrs[:, s:s + 1], bias=lnm[:, s:s + 1])
            for j in range(NB):
                bk, rb = bank()
                B.group(PE, r_tok + [r_ident], rb, [
                    (lambda s=s, j=j, bk=bk: nc.tensor.transpose(
                        out=bk[:, s * P:(s + 1) * P], in_=tokbuf[:, s, j * P:(j + 1) * P], identity=ident[:]))
                    for s in range(4)])
                act(ub[:, j, :], bk, AF.Silu, rb + [r_pp], [r_ub[j]], scale=col(pp, CNG + j), bias=col(pp, CNB + j))
            for gq in range(4):
                W, rW = ws_acquire("mix", gq)
                Wm = W[:].rearrange("p (m k c) -> p m k c", m=4, k=NB)
                for jj in range(2):
                    j = 2 * gq + jj
                    s = j % 2
                    cs = slice(jj * P, (jj + 1) * P)
                    bya, rya = bank()
                    byb, ryb = bank()
                    bga, rga = bank()
                    bgb, rgb = bank()
                    for (bk, rb, m, rhs_t, rr, off) in ((bya, rya, 0, hg, r_hg, 0), (byb, ryb, 1, ub, r_ub, 0),
                                                     (bga, rga, 2, xT, r_xT, HL), (bgb, rgb, 3, xT, r_xT, HL)):
                        B.group(PE, rr + rW, rb, [
                            (lambda kc=kc, bk=bk, m=m, rhs_t=rhs_t, off=off, cs=cs, Wm=Wm: nc.tensor.matmul(
                                bk, lhsT=Wm[:, m, kc, cs], rhs=rhs_t[:, kc, off:off + T],
                                start=(kc == 0), stop=(kc == NB - 1)))
                            for kc in range(NB)])
                    ga, r_ga = zf(6 * s)
                    gb, r_gb = zf(6 * s + 2)
                    t1, r_t1 = zf(6 * s + 4)
                    act(ga, bga, AF.Sigmoid, rga + [r_pp], r_ga, bias=col(pp, GB + j))
                    act(gb, bgb, AF.Sigmoid, rgb + [r_pp], r_gb, bias=col(pp, GB + 8 + j))
                    B.op(DVE, r_ga + rya, r_t1, lambda t1=t1, ga=ga, bya=bya: nc.vector.tensor_tensor(
                        out=t1, in0=ga, in1=bya, op=ALU.mult))
                    B.op(DVE, r_gb + ryb, ryb, lambda gb=gb, byb=byb: nc.vector.tensor_tensor(
                        out=byb, in0=gb, in1=byb, op=ALU.mult))
                    B.op(DVE, r_t1 + ryb, [r_mg[j]], lambda t1=t1, byb=byb, j=j: nc.vector.tensor_tensor(
                        out=mg[:, j, 0:T], in0=t1, in1=byb, op=ALU.add))
            W, rW = ws_acquire("wout", 0)
            Wo = W[:].rearrange("p (k c) -> p k c", c=D)
            load_tok(src, ext0 + HL)
            for s in range(4):
                pr, rpr = pair()
                for hf in range(2):
                    B.group(PE, r_mg + rW, [rpr[hf]], [
                        (lambda kc=kc, s=s, hf=hf, pr=pr, Wo=Wo: nc.tensor.matmul(
                            pr[:, hf * 512:(hf + 1) * 512], lhsT=mg[:, kc, s * P:(s + 1) * P],
                            rhs=Wo[:, kc, hf * 512:(hf + 1) * 512], start=(kc == 0), stop=(kc == NB - 1)))
                        for kc in range(NB)])
                B.op(DVE, [r_tok[s]] + rpr, [r_tok[s]], lambda s=s, pr=pr: nc.vector.scalar_tensor_tensor(
                    out=tokbuf[:, s, :], in0=tokbuf[:, s, :], scalar=ALPHA, in1=pr, op0=ALU.mult, op1=ALU.add))
                ln_stats(s, tokbuf[:, s, :], [r_tok[s]])
                act(tokbuf[:, s, :], tokbuf[:, s, :], AF.Identity, [r_tok[s], r_ln[s]], [r_tok[s]],
                    scale=lrs[:, s:s + 1], bias=lnm[:, s:s + 1])
            if g + 1 < NT_OWN:
                stage1(*tiles1[g + 1], use_y=True)
                Wn, rWn = ws_acquire("win", 0)
                stage2a(Wn, rWn)
            for j in range(NB):
                bk, rb = bank()
                B.group(PE, r_tok + [r_ident], rb, [
                    (lambda s=s, j=j, bk=bk: nc.tensor.transpose(
                        out=bk[:, s * P:(s + 1) * P], in_=tokbuf[:, s, j * P:(j + 1) * P], identity=ident[:]))
                    for s in range(4)])
                act(x1T[:, j, :], bk, AF.Identity, rb + [r_pp], [r_x1T[j]], scale=col(pp, L1G + j), bias=col(pp, L1B + j))
                act(Y[:, j, :], bk, AF.Identity, rb + [r_dp], [r_Y[j]], scale=col(dp, AG1 + j), bias=col(dp, AB1 + j))
            for gq in range(6):
                W, rW = ws_acquire("ffi", gq)
                Wf = W[:].rearrange("p (m k c) -> p m k c", m=2, k=NB)
                nblk = min(4, NFB - 4 * gq)
                for b in range(nblk):
                    f = 4 * gq + b
                    cs = slice(b * P, (b + 1) * P)
                    bg_, rbg = bank()
                    bu, rbu = bank()
                    for (bk, rb, m) in ((bg_, rbg, 0), (bu, rbu, 1)):
                        B.group(PE, r_x1T + rW, rb, [
                            (lambda kc=kc, bk=bk, m=m, cs=cs, Wf=Wf: nc.tensor.matmul(
                                bk, lhsT=Wf[:, m, kc, cs], rhs=x1T[:, kc, :], start=(kc == 0), stop=(kc == NB - 1)))
                            for kc in range(NB)])
                    sg, r_sg = zf(22 + 2 * (f % 2))
                    act(sg, bg_, AF.Silu, rbg, r_sg)
                    hT, r_hT = zb(f)
                    B.op(DVE, r_sg + rbu, r_hT, lambda hT=hT, sg=sg, bu=bu: nc.vector.tensor_tensor(
                        out=hT, in0=sg, in1=bu, op=ALU.mult))
            for gq in range(4):
                W, rW = ws_acquire("ffo", gq)
                Wq = W[:, 0:NFB * 256].rearrange("p (f c) -> p f c", c=256)
                for jj in range(2):
                    j = 2 * gq + jj
                    bk, rb = bank()
                    B.group(PE, r_Z[0:NFB] + rW, rb, [
                        (lambda f=f, bk=bk, jj=jj, Wq=Wq: nc.tensor.matmul(
                            bk, lhsT=Wq[:, f, jj * P:(jj + 1) * P], rhs=Z[:, f, :], start=(f == 0), stop=(f == NFB - 1)))
                        for f in range(NFB)])
                    B.op(DVE, [r_Y[j]] + rb, [r_Y[j]], lambda j=j, bk=bk: nc.vector.tensor_tensor(
                        out=Y[:, j, :], in0=Y[:, j, :], in1=bk, op=ALU.add))
            for s in range(4):
                pr, rpr = pair()
                B.group(PE, r_Y + [r_ident], rpr, [
                    (lambda j=j, s=s, pr=pr: nc.tensor.transpose(
                        out=pr[:, j * P:(j + 1) * P], in_=Y[:, j, s * P:(s + 1) * P], identity=ident[:]))
                    for j in range(NB)])
                ln_stats(s, pr, rpr)
                act(tokbuf[:, s, :], pr, AF.Identity, rpr + [r_ln[s]], [r_tok[s]],
                    scale=lrs[:, s:s + 1], bias=lnm[:, s:s + 1])
                B.op(DVE, [r_tok[s], r_lnb], [r_tok[s]], lambda s=s: nc.vector.tensor_tensor(
                    out=tokbuf[:, s, :], in0=tokbuf[:, s, :], in1=lnb[:, 0, :], op=ALU.mult))
                B.op(DVE, [r_tok[s], r_lnb], [r_tok[s]], lambda s=s: nc.vector.tensor_tensor(
                    out=tokbuf[:, s, :], in0=tokbuf[:, s, :], in1=lnb[:, 1, :], op=ALU.add))
                B.dma(POOL, d_to[s], [r_tok[s]], [r_y], [lambda s=s, out0=out0: nc.gpsimd.dma_start(
                    out=y[out0 + s * P:out0 + (s + 1) * P, :], in_=tokbuf[:, s, :])])

        B._wait(SP, [(d, d.n) for d in d_to])
        B._wait(POOL, [(d, d.n) for d in d_to])
    return nc


_NC = None


def _prep_shared(inp):
    f = lambda a: np.ascontiguousarray(np.asarray(a, dtype=np.float32))
    ch = lambda v: f(v).reshape(NB, P).T
    pp = np.zeros((P, NPP), np.float32)
    pp[:, CW4:CW4 + 32] = f(inp["rnn_conv_w"])[0].reshape(4, NB, P).transpose(2, 1, 0).reshape(P, 32)
    pp[:, CB4:CB4 + 8] = ch(inp["rnn_conv_b"][0])
    for d in range(2):
        pp[:, BA + d * 8:BA + d * 8 + 8] = ch(inp["lru_ba"][0][d])
        pp[:, BX + d * 8:BX + d * 8 + 8] = ch(inp["lru_bx"][0][d])
        pp[:, LAM + d * 8:LAM + d * 8 + 8] = ch(inp["lru_lambda"][0][d])
        pp[:, GB + d * 8:GB + d * 8 + 8] = ch(inp["gate_b"][0][d])
    pp[:, CW31:CW31 + 248] = f(inp["conf_conv_w"])[0].reshape(31, NB, P).transpose(2, 1, 0).reshape(P, 248)
    pp[:, CB31:CB31 + 8] = ch(inp["conf_conv_b"][0])
    pp[:, CNG:CNG + 8] = ch(inp["conf_norm_g"][0])
    pp[:, CNB:CNB + 8] = ch(inp["conf_norm_b"][0])
    pp[:, L1G:L1G + 8] = ch(inp["ln1_g"][0])
    pp[:, L1B:L1B + 8] = ch(inp["ln1_b"][0])
    wa = f(inp["lru_wa"])[0]
    wx = f(inp["lru_wx"])[0]
    lw = np.stack([wa[0], wx[0], wa[1], wx[1]], axis=0)
    lruw = np.ascontiguousarray(lw.transpose(2, 0, 1, 3)).reshape(P, 4 * NB * P)
    lnb = np.ascontiguousarray(np.broadcast_to(
        np.stack([f(inp["ln2_g"])[0], f(inp["ln2_b"])[0]], axis=0)[None], (P, 2, D)))
    return dict(
        w_in=f(inp["w_in"])[0], w_rp=f(inp["w_rnn_proj"])[0], w_cp=f(inp["w_conf_proj"])[0], w_o=f(inp["w_out"])[0],
        w_fi=f(inp["w_ffn_in"])[0], w_fo=f(inp["w_ffn_out"])[0], lruw=lruw, pp=pp, lnb=lnb,
        idn=np.eye(P, dtype=np.float32))


def kernel(**inputs):
    global _NC
    xp = np.asarray(inputs["x_prompt"], dtype=np.float32)
    xsm = np.asarray(inputs["x_sample"], dtype=np.float32)[0]
    shared = _prep_shared(inputs)
    xsamp = np.zeros((SAMP + 2 * HL, D), np.float32)
    xsamp[HL:HL + SAMP] = xsm
    xwin = np.stack([xsamp[u * T:u * T + TC] for u in range(SAMP // T)], axis=0)
    in_maps = []
    for c in range(8):
        xs = np.zeros((XS_ROWS, D), np.float32)
        for q in range(2):
            r0 = q * (SEQ + 2 * HL)
            xs[r0 + HL:r0 + HL + SEQ] = xp[2 * c + q]
        r0 = 2 * (SEQ + 2 * HL)
        xs[r0:r0 + CH + 2 * HL] = xsamp[c * CH:c * CH + CH + 2 * HL]
        m = np.zeros((P, 4, NT_S), np.float32)
        u = np.arange(NT_S)
        mf = (u < 4 * c).astype(np.float32)
        mb = (u >= 4 * c).astype(np.float32)
        m[:, 0, :] = mf
        m[:, 1, :] = 1.0 - mf
        m[:, 2, :] = mb
        m[:, 3, :] = 1.0 - mb
        d = dict(shared)
        d["xs"] = xs
        others = [t for t in range(SAMP // T) if not (4 * c <= t < 4 * c + 4)]
        d["xsamp"] = np.ascontiguousarray(xwin[others]).reshape(NT_S * TC, D)
        d["msk"] = m
        in_maps.append(d)
    if _NC is None:
        _NC = build_program()
    res = run_bass_kernel_spmd(_NC, in_maps, core_ids=list(range(8)))
    y_prompt = np.zeros((16, SEQ, D), np.float32)
    y_sample = np.zeros((1, SAMP, D), np.float32)
    for c in range(8):
        yc = np.asarray(res.results[c]["y"], dtype=np.float32)
        y_prompt[2 * c] = yc[0:SEQ]
        y_prompt[2 * c + 1] = yc[SEQ:2 * SEQ]
        y_sample[0, c * CH:(c + 1) * CH] = yc[2 * SEQ:2 * SEQ + CH]
    return (y_prompt, y_sample)
```

```python
import numpy as np
from contextlib import ExitStack
import concourse.bass as bass
import concourse.mybir as mybir
from concourse.bass_utils import run_bass_kernel_spmd

F32 = mybir.dt.float32
BF16 = mybir.dt.bfloat16
AF = mybir.ActivationFunctionType
ALU = mybir.AluOpType

P = 128
D = 1024
NB = 8
T = 512
HL = 16
TC = T + 2 * HL
HC = TC // 2
DFF = 2816
NFB = DFF // P
SEQ = 4096
SAMP = 16384
CH = SAMP // 8
ALPHA = 2.0 ** 0.25
EPS = 1e-5
NT_OWN = 20
NT_S = SAMP // T - CH // T
NT1 = NT_OWN + NT_S
XS_ROWS = 2 * (SEQ + 2 * HL) + CH + 2 * HL
NSLOT = 3
SLOT_ELEMS = 8192

CW4, CB4, BA, BX, LAM, CW31, CB31, CNG, CNB, GB, L1G, L1B = 0, 32, 40, 56, 72, 88, 336, 344, 352, 360, 376, 384
NPP = 392
HBA, HBX, KK, HK, HCB4, K256, AG1, AB1, HCW31 = 0, 16, 32, 48, 64, 72, 88, 96, 104
NDP = 352


class Reg:
    __slots__ = ("w", "rs")

    def __init__(self):
        self.w = {}
        self.rs = {}


class Sem:
    def __init__(self, h):
        self.h = h
        self.n = 0


class Eng:
    def __init__(self, h, sem):
        self.h = h
        self.sem = sem
        self.waited = {}


class Builder:
    def __init__(self, nc, es):
        self.nc = nc
        self.es = es
        self.nsem = 0
        self.allsems = []
        self.PE = Eng(nc.tensor, self.sem("pe"))
        self.ACT = Eng(nc.scalar, self.sem("act"))
        self.DVE = Eng(nc.vector, self.sem("dve"))
        self.POOL = Eng(nc.gpsimd, self.sem("pool"))
        self.SP = Eng(nc.sync, None)

    def sem(self, name):
        self.nsem += 1
        sm = Sem(self.es.enter_context(self.nc.semaphore("s_%s_%d" % (name, self.nsem))))
        self.allsems.append(sm)
        return sm

    def barrier(self):
        toks = [(sm, sm.n) for sm in self.allsems if sm.n > 0]
        for e in (self.PE, self.ACT, self.DVE, self.POOL, self.SP):
            self._wait(e, toks)

    def _wait(self, eng, toks):
        best = {}
        for (s, v) in toks:
            if v > best.get(s, 0):
                best[s] = v
        for s, v in best.items():
            if eng.waited.get(s, 0) >= v:
                continue
            eng.h.wait_ge(s.h, v)
            eng.waited[s] = v

    @staticmethod
    def _deps(reads, writes):
        toks = []
        for r in reads:
            toks.extend(r.w.items())
        for w in writes:
            toks.extend(w.w.items())
            toks.extend(w.rs.items())
        return toks

    @staticmethod
    def _commit(tok, reads, writes):
        s, v = tok
        for r in reads:
            if r.rs.get(s, 0) < v:
                r.rs[s] = v
        for w in writes:
            if w.w.get(s, 0) < v:
                w.w[s] = v
            w.rs = {}

    def op(self, eng, reads, writes, fn):
        self._wait(eng, self._deps(reads, writes))
        ins = fn()
        eng.sem.n += 1
        ins.then_inc(eng.sem.h, 1)
        self._commit((eng.sem, eng.sem.n), reads, writes)

    def group(self, eng, reads, writes, fns):
        self._wait(eng, self._deps(reads, writes))
        ins = None
        for f in fns:
            ins = f()
        eng.sem.n += 1
        ins.then_inc(eng.sem.h, 1)
        self._commit((eng.sem, eng.sem.n), reads, writes)

    def dma(self, eng, dsem, reads, writes, fns):
        self._wait(eng, self._deps(reads, writes))
        for f in fns:
            ins = f()
            dsem.n += 16
            ins.then_inc(dsem.h, 16)
        self._commit((dsem, dsem.n), reads, writes)


def build_program():
    nc = bass.Bass("TRN2", target_bir_lowering=False)
    dt = nc.dram_tensor
    xs = dt("xs", [XS_ROWS, D], F32, kind="ExternalInput").ap()
    xsamp = dt("xsamp", [NT_S * TC, D], F32, kind="ExternalInput").ap()
    w_in = dt("w_in", [D, 6 * D], F32, kind="ExternalInput").ap()
    w_rp = dt("w_rp", [D, D], F32, kind="ExternalInput").ap()
    w_cp = dt("w_cp", [D, D], F32, kind="ExternalInput").ap()
    w_o = dt("w_o", [D, D], F32, kind="ExternalInput").ap()
    w_fi = dt("w_fi", [D, 2 * DFF], F32, kind="ExternalInput").ap()
    w_fo = dt("w_fo", [DFF, D], F32, kind="ExternalInput").ap()
    lruw_d = dt("lruw", [P, 4 * NB * P], F32, kind="ExternalInput").ap()
    pp_d = dt("pp", [P, NPP], F32, kind="ExternalInput").ap()
    lnb_d = dt("lnb", [P, 2, D], F32, kind="ExternalInput").ap()
    msk_d = dt("msk", [P, 4, NT_S], F32, kind="ExternalInput").ap()
    idn_d = dt("idn", [P, P], F32, kind="ExternalInput").ap()
    y = dt("y", [NT_OWN * T, D], F32, kind="ExternalOutput").ap()
    w_in_b = dt("w_in_b", [D, 6 * D], BF16).ap()
    w_rp_b = dt("w_rp_b", [D, D], BF16).ap()
    w_cp_b = dt("w_cp_b", [D, D], BF16).ap()
    w_o_b = dt("w_o_b", [D, D], BF16).ap()
    w_fi_b = dt("w_fi_b", [D, 2 * DFF], BF16).ap()
    w_fo_b = dt("w_fo_b", [DFF, D], BF16).ap()
    dgs = dt("dgs", [NB, P, 31 * P], BF16).ap()

    with ExitStack() as es:
        B = Builder(nc, es)
        PE, ACT, DVE, POOL, SP = B.PE, B.ACT, B.DVE, B.POOL, B.SP

        def sb(name, shape, dtype=F32):
            return es.enter_context(nc.sbuf_tensor("sb_" + name, shape, dtype))

        ident = sb("ident", [P, P]); identb = sb("identb", [P, P], BF16)
        lruw = sb("lruw", [P, 4, NB, P], BF16)
        dg4 = sb("dg4", [P, NB, 4, P], BF16)
        dg31 = sb("dg31", [P, 2, 31, P], BF16)
        pp = sb("pp", [P, NPP]); dp = sb("dp", [P, NDP])
        lnb = sb("lnb", [P, 2, D])
        msk = sb("msk", [P, 4, NT_S])
        SR = sb("SR", [P, NT1, 16]); BBt = sb("BB", [P, NT1, 16]); AA = SR
        CAR = sb("CAR", [P, NT_OWN, 16])
        fold = sb("fold", [P, 4, 16])
        mhalf = sb("mhalf", [P, 1])
        tmp16 = sb("tmp16", [P, 6, 16])
        lst = sb("lst", [P, 4, 2, 6]); lmv = sb("lmv", [P, 4, 2]); lve = sb("lve", [P, 4])
        lrs = sb("lrs", [P, 4]); lnm = sb("lnm", [P, 4])
        wsl = [sb("wslot%d" % i, [P, SLOT_ELEMS], BF16) for i in range(NSLOT)]
        tokbuf = sb("tokbuf", [P, 4, D])
        xh = sb("xh", [32, D])
        xT = sb("xT", [P, NB, TC], BF16)
        xrn = sb("xrn", [P, NB, TC], BF16)
        hg = sb("hg", [P, NB, T], BF16)
        ub = sb("ub", [P, NB, T], BF16)
        Y = sb("Y", [P, NB, T])
        Z = sb("Z", [P, 30, T], BF16)
        ps = es.enter_context(nc.psum_tensor("ps", [P, 8, T], F32))

        R = lambda: Reg()
        r_ident, r_identb, r_lruw, r_dg4, r_pp, r_dp, r_lnb, r_msk = R(), R(), R(), R(), R(), R(), R(), R()
        r_dg31 = [R(), R()]
        r_SR, r_BB, r_CAR, r_fold, r_mhalf, r_tmp16 = R(), R(), R(), R(), R(), R()
        r_AA = r_SR
        r_ln = [R() for _ in range(4)]
        r_wsl = [R() for _ in range(NSLOT)]
        r_tok = [R() for _ in range(4)]
        r_xh = R()
        r_xT = [R() for _ in range(NB)]
        r_xrn = [R() for _ in range(NB)]
        r_hg = [R() for _ in range(NB)]
        r_ub = [R() for _ in range(NB)]
        r_Y = [R() for _ in range(NB)]
        r_Z = [R() for _ in range(30)]
        r_ps = [R() for _ in range(8)]
        r_win = [R() for _ in range(6)]
        r_wrp, r_wcp, r_wo, r_wfi, r_wfo = R(), R(), R(), R(), R()
        r_y = R()

        d_const = B.sem("dc")
        d_ws = [B.sem("dws") for _ in range(NSLOT)]
        d_tl = [B.sem("dtl") for _ in range(4)]
        d_to = [B.sem("dto") for _ in range(4)]
        d_xh = B.sem("dxh")
        d_ty = [B.sem("dty") for _ in range(4)]
        d_dgw = [B.sem("ddgw"), B.sem("ddgw")]
        d_dgr = [B.sem("ddgr"), B.sem("ddgr")]
        r_dgs = [R() for _ in range(NB)]
        d_ci = [B.sem("dci"), B.sem("dci")]
        d_co = [B.sem("dco"), B.sem("dco")]

        def zf(u, n=1):
            return Z[:, u:u + 2 * n, :].rearrange("p a b -> p (a b)").bitcast(F32), r_Z[u:u + 2 * n]

        def zb(u, n=1):
            return Z[:, u:u + n, :].rearrange("p a b -> p (a b)"), r_Z[u:u + n]

        psc = [0]

        ps_split = [False]
        gsc = [0]

        def bank():
            if ps_split[0]:
                i = psc[0] % 4
            else:
                i = psc[0] % 8
            psc[0] += 1
            return ps[:, i, :], [r_ps[i]]

        def gbank():
            if not ps_split[0]:
                return bank()
            i = 4 + gsc[0] % 4
            gsc[0] += 1
            return ps[:, i, :], [r_ps[i]]

        def pair():
            if psc[0] % 2:
                psc[0] += 1
            i = psc[0] % 8
            psc[0] += 2
            return ps[:, i:i + 2, :].rearrange("p a b -> p (a b)"), [r_ps[i], r_ps[i + 1]]

        evc = [0]
        evc_dve_only = [False]
        evc_force = [None]

        def evac(out, in_, reads, writes):
            evc[0] += 1
            if evc_force[0] == "act" or (evc_force[0] is None and evc[0] % 2 and not evc_dve_only[0]):
                B.op(ACT, reads, writes, lambda: nc.scalar.copy(out=out, in_=in_))
            else:
                B.op(DVE, reads, writes, lambda: nc.vector.tensor_copy(out=out, in_=in_))

        def act(out, in_, func, reads, writes, scale=1.0, bias=0.0):
            B.op(ACT, reads, writes,
                 lambda: nc.scalar.activation(out=out, in_=in_, func=func, bias=bias, scale=scale))

        def col(t, c):
            return t[:, c:c + 1]

        B.dma(SP, d_const, [], [r_ident], [lambda: nc.sync.dma_start(out=ident[:], in_=idn_d)])
        B.dma(SP, B.sem("dc2"), [], [r_pp], [lambda: nc.sync.dma_start(out=pp[:], in_=pp_d)])
        B.dma(SP, B.sem("dc4"), [], [r_msk], [lambda: nc.sync.dma_start(out=msk[:], in_=msk_d)])
        B.op(DVE, [r_ident], [r_identb], lambda: nc.vector.tensor_copy(out=identb[:], in_=ident[:]))
        B.op(DVE, [], [r_mhalf], lambda: nc.vector.memset(mhalf[:], -0.5))
        B.op(DVE, [], [r_CAR], lambda: nc.vector.memset(CAR[:], 0.0))
        B.op(DVE, [], [r_SR], lambda: nc.vector.memset(SR[:], 0.0))

        def ts(out, in0, s1, op0, s2=None, op1=None, reads=(), writes=()):
            if op1 is None:
                B.op(DVE, list(reads), list(writes),
                     lambda: nc.vector.tensor_scalar(out=out, in0=in0, scalar1=s1, scalar2=None, op0=op0))
            else:
                B.op(DVE, list(reads), list(writes),
                     lambda: nc.vector.tensor_scalar(out=out, in0=in0, scalar1=s1, scalar2=s2, op0=op0, op1=op1))

        ts(dp[:, HBA:HBA + 16], pp[:, BA:BA + 16], 0.5, ALU.mult, reads=[r_pp], writes=[r_dp])
        ts(dp[:, HBX:HBX + 16], pp[:, BX:BX + 16], 0.5, ALU.mult, reads=[r_pp], writes=[r_dp])
        ts(dp[:, HCB4:HCB4 + 8], pp[:, CB4:CB4 + 8], 0.5, ALU.mult, reads=[r_pp], writes=[r_dp])
        ts(dp[:, HCW31:HCW31 + 248], pp[:, CW31:CW31 + 248], 0.5, ALU.mult, reads=[r_pp], writes=[r_dp])
        ts(dp[:, AG1:AG1 + 8], pp[:, L1G:L1G + 8], ALPHA, ALU.mult, reads=[r_pp], writes=[r_dp])
        ts(dp[:, AB1:AB1 + 8], pp[:, L1B:L1B + 8], ALPHA, ALU.mult, reads=[r_pp], writes=[r_dp])
        t_x, t_ax, t_e, t_l, t_m, t_sp = (tmp16[:, i, :] for i in range(6))
        ts(t_x, pp[:, LAM:LAM + 16], -1.0, ALU.mult, reads=[r_pp], writes=[r_tmp16])
        B.op(DVE, [r_tmp16, r_pp], [r_tmp16], lambda: nc.vector.tensor_tensor(
            out=t_ax, in0=t_x, in1=pp[:, LAM:LAM + 16], op=ALU.max))
        act(t_e, t_ax, AF.Exp, [r_tmp16], [r_tmp16], scale=-1.0)
        act(t_l, t_e, AF.Ln, [r_tmp16], [r_tmp16], scale=1.0, bias=1.0)
        ts(t_m, t_x, 0.0, ALU.max, reads=[r_tmp16], writes=[r_tmp16])
        B.op(DVE, [r_tmp16], [r_tmp16], lambda: nc.vector.tensor_tensor(out=t_sp, in0=t_m, in1=t_l, op=ALU.add))
        ts(dp[:, KK:KK + 16], t_sp, -8.0, ALU.mult, reads=[r_tmp16], writes=[r_dp])
        ts(dp[:, HK:HK + 16], t_sp, -4.0, ALU.mult, reads=[r_tmp16], writes=[r_dp])
        ts(dp[:, K256:K256 + 16], t_sp, -8.0 * (T / 2), ALU.mult, reads=[r_tmp16], writes=[r_dp])
        B.op(POOL, [r_identb, r_pp], [r_dg4], lambda: nc.gpsimd.tensor_tensor(
            out=dg4[:].rearrange("p a k q -> p (a k) q"),
            in0=identb[:].unsqueeze(1).to_broadcast([P, NB * 4, P]),
            in1=pp[:, CW4:CW4 + 32].unsqueeze(2).to_broadcast([P, NB * 4, P]), op=ALU.mult))

        castin = Y[:].rearrange("p (a b) c -> p a (b c)", a=2)
        castout = ub[:].rearrange("p (a b) c -> p a (b c)", a=2)
        r_cin = [r_Y[0:4], r_Y[4:8]]
        r_cout = [r_ub[0:4], r_ub[4:8]]
        B.dma(POOL, d_ci[0], [], r_cin[0] + r_cin[1],
              [lambda: nc.gpsimd.dma_start(out=Y[:].rearrange("p a b -> p (a b)"), in_=lruw_d)])
        B.op(POOL, r_cin[0] + r_cin[1], [r_lruw], lambda: nc.gpsimd.tensor_copy(
            out=lruw[:].rearrange("p a b c -> p (a b c)"), in_=Y[:].rearrange("p a b -> p (a b)")))
        chunks = []
        for cg in range(6):
            for kc in range(NB):
                chunks.append((w_in, w_in_b, kc, cg * D, D, r_win[cg]))
        for (src, dst, reg) in ((w_rp, w_rp_b, r_wrp), (w_cp, w_cp_b, r_wcp), (w_o, w_o_b, r_wo)):
            for kc in range(NB):
                chunks.append((src, dst, kc, 0, D, reg))
        for kc in range(NB):
            for (c0, n) in ((0, 2048), (2048, 2048), (4096, 1536)):
                chunks.append((w_fi, w_fi_b, kc, c0, n, r_wfi))
        for f in range(NFB):
            chunks.append((w_fo, w_fo_b, f, 0, D, r_wfo))

        def cast_load(i):
            src, dst, kc, c0, n, reg = chunks[i]
            b = i % 2
            B.dma(POOL, d_ci[b], [], r_cin[b], [lambda: nc.gpsimd.dma_start(
                out=castin[:, b, 0:n], in_=src[kc * P:(kc + 1) * P, c0:c0 + n])])

        def cast_do(i):
            src, dst, kc, c0, n, reg = chunks[i]
            b = i % 2
            B.op(POOL, r_cin[b], r_cout[b],
                 lambda: nc.gpsimd.tensor_copy(out=castout[:, b, 0:n], in_=castin[:, b, 0:n]))
            B.dma(POOL, d_co[b], r_cout[b], [reg], [lambda: nc.gpsimd.dma_start(
                out=dst[kc * P:(kc + 1) * P, c0:c0 + n], in_=castout[:, b, 0:n])])

        cast_load(0)
        cast_load(1)
        cast_pos = [0]

        def cast_steps(n):
            for _ in range(n):
                i = cast_pos[0]
                if i >= len(chunks):
                    return
                cast_do(i)
                if i + 2 < len(chunks):
                    cast_load(i + 2)
                cast_pos[0] += 1

        cast_steps(16)
        plan = [("win", 0), ("win", 0)]
        for t_ in range(NT_OWN):
            plan += [("win", 1), ("win", 3), ("win", 2)]
            plan += [("mix", g) for g in range(4)]
            plan += [("wout", 0)]
            if t_ + 1 < NT_OWN:
                plan += [("win", 0)]
            plan += [("ffi", g) for g in range(6)]
            plan += [("ffo", g) for g in range(4)]
        wst = {"issued": 0, "next": 0}

        def ws_issue(i):
            kind, g = plan[i]
            s = i % NSLOT
            W = wsl[s]
            if kind == "win":
                fns = [lambda: nc.sync.dma_start(
                    out=W[:].rearrange("p (k c) -> p k c", c=D),
                    in_=w_in_b.rearrange("(k p) c -> p k c", p=P)[:, :, g * D:(g + 1) * D])]
                srcs = [r_win[g]]
            elif kind == "wout":
                fns = [lambda: nc.sync.dma_start(
                    out=W[:].rearrange("p (k c) -> p k c", c=D),
                    in_=w_o_b.rearrange("(k p) c -> p k c", p=P))]
                srcs = [r_wo]
            elif kind == "mix":
                Wv = W[:].rearrange("p (m k c) -> p m k c", m=4, k=NB)
                c0 = g * 256
                fns = [
                    lambda: nc.sync.dma_start(out=Wv[:, 0], in_=w_rp_b.rearrange("(k p) c -> p k c", p=P)[:, :, c0:c0 + 256]),
                    lambda: nc.sync.dma_start(out=Wv[:, 1], in_=w_cp_b.rearrange("(k p) c -> p k c", p=P)[:, :, c0:c0 + 256]),
                    lambda: nc.sync.dma_start(out=Wv[:, 2], in_=w_in_b.rearrange("(k p) c -> p k c", p=P)[:, :, 4 * D + c0:4 * D + c0 + 256]),
                    lambda: nc.sync.dma_start(out=Wv[:, 3], in_=w_in_b.rearrange("(k p) c -> p k c", p=P)[:, :, 5 * D + c0:5 * D + c0 + 256]),
                ]
                srcs = [r_wrp, r_wcp, r_win[4], r_win[5]]
            elif kind == "ffi":
                Wv = W[:].rearrange("p (m k c) -> p m k c", m=2, k=NB)
                c0 = g * 512
                n = min(512, DFF - c0)
                fns = [
                    lambda: nc.sync.dma_start(out=Wv[:, 0, :, 0:n], in_=w_fi_b.rearrange("(k p) c -> p k c", p=P)[:, :, c0:c0 + n]),
                    lambda: nc.sync.dma_start(out=Wv[:, 1, :, 0:n], in_=w_fi_b.rearrange("(k p) c -> p k c", p=P)[:, :, DFF + c0:DFF + c0 + n]),
                ]
                srcs = [r_wfi]
            else:
                Wv = W[:, 0:NFB * 256].rearrange("p (f c) -> p f c", c=256)
                fns = [lambda: nc.sync.dma_start(
                    out=Wv, in_=w_fo_b.rearrange("(f p) c -> p f c", p=P)[:, :, g * 256:(g + 1) * 256])]
                srcs = [r_wfo]
            B.dma(SP, d_ws[s], srcs, [r_wsl[s]], fns)

        def ws_acquire(kind, g, hold=0):
            n = wst["next"]
            assert plan[n] == (kind, g), (plan[n], kind, g)
            wst["next"] += 1
            while wst["issued"] < min(len(plan), n + NSLOT - hold):
                ws_issue(wst["issued"])
                wst["issued"] += 1
            s = n % NSLOT
            return wsl[s], [r_wsl[s]]

        def load_tok(src, row0):
            for s in range(4):
                B.dma(SP, d_tl[s], [], [r_tok[s]], [lambda s=s: nc.sync.dma_start(
                    out=tokbuf[:, s, :], in_=src[row0 + s * P:row0 + (s + 1) * P, :])])

        Ystage = Y[:].rearrange("p (s b) c -> p s (b c)", s=4)

        def stage1(src, ext0, use_y=False):
            if use_y:
                xst, r_xst = Ystage, r_Y
                for s in range(4):
                    B.dma(SP, d_ty[s], [], r_Y[2 * s:2 * s + 2], [lambda s=s: nc.sync.dma_start(
                        out=Ystage[:, s, :], in_=src[ext0 + HL + s * P:ext0 + HL + (s + 1) * P, :])])
            else:
                xst, r_xst = tokbuf, r_tok
                load_tok(src, ext0 + HL)
            B.dma(SP, d_xh, [], [r_xh], [
                lambda: nc.sync.dma_start(out=xh[0:HL, :], in_=src[ext0:ext0 + HL, :]),
                lambda: nc.sync.dma_start(out=xh[HL:2 * HL, :], in_=src[ext0 + HL + T:ext0 + TC, :])])
            for c in range(NB):
                bk, rb = bank()
                B.group(PE, r_xst + [r_ident], rb, [
                    (lambda s=s, c=c, bk=bk: nc.tensor.transpose(
                        out=bk[:, s * P:(s + 1) * P], in_=xst[:, s, c * P:(c + 1) * P], identity=ident[:]))
                    for s in range(4)])
                evac(xT[:, c, HL:HL + T], bk, rb, [r_xT[c]])
            bk, rb = bank()
            B.group(PE, [r_xh, r_ident], rb, [
                (lambda c=c, bk=bk: nc.tensor.transpose(
                    out=bk[:, c * 32:(c + 1) * 32], in_=xh[0:32, c * P:(c + 1) * P], identity=ident[0:32, 0:32]))
                for c in range(NB)])
            hv = bk[:, 0:256].rearrange("p (a b) -> p a b", b=32)
            evac(xT[:, :, 0:HL], hv[:, :, 0:HL], rb, r_xT)
            evac(xT[:, :, HL + T:TC], hv[:, :, HL:2 * HL], rb, r_xT)

        def stage2a(W, rW):
            Wv = W[:].rearrange("p (k c) -> p k c", c=D)
            for j in range(NB):
                for h in range(2):
                    bk, rb = bank()
                    B.group(PE, r_xT + rW, rb, [
                        (lambda kc=kc, j=j, h=h, bk=bk: nc.tensor.matmul(
                            bk[:, 0:HC], lhsT=Wv[:, kc, j * P:(j + 1) * P], rhs=xT[:, kc, h * HC:(h + 1) * HC],
                            start=(kc == 0), stop=(kc == NB - 1)))
                        for kc in range(NB)])
                    evac(xrn[:, j, h * HC:(h + 1) * HC], bk[:, 0:HC], rb, [r_xrn[j]])

        def stage2b(j, g1, pass2, car_g, mid=None):
            s = j % 2
            base = 11 * s
            a_d = [zf(base + 0), zf(base + 4)]
            t_d = [zf(base + 2), zf(base + 6)]
            xch, r_xch = zf(base + 8)
            xcb, r_xcb = zb(base + 10)
            bc, rbc = bank()
            B.group(PE, [r_xrn[j], r_dg4], rbc, [
                (lambda k=k: nc.tensor.matmul(bc, lhsT=dg4[:, j, k, :], rhs=xrn[:, j, HL - 2 + k:HL - 2 + k + T],
                                              start=(k == 0), stop=(k == 3)))
                for k in range(4)])
            B.op(DVE, rbc + [r_pp], r_xcb, lambda: nc.vector.tensor_scalar(
                out=xcb, in0=bc, scalar1=col(pp, CB4 + j), scalar2=None, op0=ALU.add))
            B.op(DVE, rbc + [r_dp], r_xch, lambda: nc.vector.tensor_scalar(
                out=xch, in0=bc, scalar1=0.5, scalar2=col(dp, HCB4 + j), op0=ALU.mult, op1=ALU.add))
            if mid is not None:
                mid()
            banks = []
            for d in range(2):
                br, rbr = gbank()
                bi, rbi = gbank()
                B.group(PE, r_xcb + [r_lruw], rbr, [lambda br=br, d=d: nc.tensor.matmul(
                    br, lhsT=lruw[:, 2 * d, j, :], rhs=xcb, start=True, stop=True)])
                B.group(PE, r_xcb + [r_lruw], rbi, [lambda bi=bi, d=d: nc.tensor.matmul(
                    bi, lhsT=lruw[:, 2 * d + 1, j, :], rhs=xcb, start=True, stop=True)])
                banks.append((br, rbr, bi, rbi))
            for d in range(2):
                br, rbr, bi, rbi = banks[d]
                cdx = d * 8 + j
                if pass2:
                    act(br, br, AF.Tanh, rbr + [r_dp], rbr, scale=0.5, bias=col(dp, HBA + cdx))
                else:
                    B.op(ACT, rbr + [r_dp], rbr + [r_SR], lambda br=br, cdx=cdx: nc.scalar.activation(
                        out=br, in_=br, func=AF.Tanh, bias=col(dp, HBA + cdx), scale=0.5,
                        accum_out=SR[:, g1, cdx:cdx + 1]))
                act(bi, bi, AF.Tanh, rbi + [r_dp], rbi, scale=0.5, bias=col(dp, HBX + cdx))
                act(a_d[d][0], br, AF.Exp, rbr + [r_dp], a_d[d][1], scale=col(dp, HK + cdx), bias=col(dp, HK + cdx))
                act(br, br, AF.Exp, rbr + [r_dp], rbr, scale=col(dp, KK + cdx), bias=col(dp, KK + cdx))
                B.op(DVE, rbr, rbr, lambda br=br: nc.vector.tensor_scalar(
                    out=br, in0=br, scalar1=0.99999994, scalar2=None, op0=ALU.min))
            for d in range(2):
                br, rbr, bi, rbi = banks[d]
                act(br, br, AF.Sqrt, rbr, rbr, scale=-1.0, bias=1.0)
            for d in range(2):
                br, rbr, bi, rbi = banks[d]
                cdx = d * 8 + j
                tt, rtt = t_d[d]
                aa, raa = a_d[d]
                B.op(DVE, rbi + r_xch, rtt, lambda tt=tt, bi=bi: nc.vector.scalar_tensor_tensor(
                    out=tt, in0=bi, scalar=1.0, in1=xch, op0=ALU.add, op1=ALU.mult))
                B.op(DVE, rtt + rbr, rtt, lambda tt=tt, br=br: nc.vector.tensor_tensor(
                    out=tt, in0=tt, in1=br, op=ALU.mult))
                init = CAR[:, car_g, cdx:cdx + 1] if pass2 else 0.0
                rinit = [r_CAR] if pass2 else []
                if d == 0:
                    B.op(DVE, rtt + raa + rinit, rtt, lambda tt=tt, aa=aa, init=init: nc.vector.tensor_tensor_scan(
                        out=tt, data0=aa, data1=tt, initial=init, op0=ALU.mult, op1=ALU.add))
                else:
                    B.op(DVE, rtt + raa + rinit, rtt, lambda tt=tt, aa=aa, init=init: nc.vector.tensor_tensor_scan(
                        out=tt[:, ::-1], data0=aa[:, ::-1], data1=tt[:, ::-1], initial=init,
                        op0=ALU.mult, op1=ALU.add))
                if not pass2:
                    endc = T - 1 if d == 0 else 0
                    B.op(DVE, rtt, [r_BB], lambda tt=tt, endc=endc, cdx=cdx: nc.vector.tensor_copy(
                        out=BBt[:, g1, cdx:cdx + 1], in_=tt[:, endc:endc + 1]))
            if pass2:
                tf, rtf = t_d[0]
                tb, rtb = t_d[1]
                B.op(POOL, rtf + rtb, rtf, lambda: nc.gpsimd.tensor_tensor(out=tf, in0=tf, in1=tb, op=ALU.add))
                B.op(POOL, rtf + [r_Y[j]], [r_hg[j]], lambda: nc.gpsimd.tensor_tensor(
                    out=hg[:, j, :], in0=tf, in1=Y[:, j, :], op=ALU.mult))


        A2 = dg31[:].rearrange("p a k q -> p (a k q)")
        r_A2 = [R() for _ in range(15)]
        lnbf = lnb[:].rearrange("p a d -> p (a d)")
        r_L = [R() for _ in range(4)]

        def a2f(u):
            return A2[:, u * T:(u + 2) * T].bitcast(F32), r_A2[u:u + 2]

        def hgf(u):
            return hg[:, u:u + 2, :].rearrange("p a b -> p (a b)").bitcast(F32), r_hg[u:u + 2]

        def p1_views(j):
            s = j % 4
            if s < 2:
                base = 11 * s
                return dict(a=[zf(base + 0), zf(base + 4)], t=[zf(base + 2), zf(base + 6)], xch=zf(base + 8),
                            xcb=zb(base + 10), q=[zf(22 + 4 * s), zf(24 + 4 * s)])
            if s == 2:
                return dict(a=[a2f(0), a2f(4)], t=[a2f(2), a2f(6)], xch=a2f(8),
                            xcb=(A2[:, 10 * T:11 * T], r_A2[10:11]), q=[a2f(11), a2f(13)])
            return dict(a=[hgf(0), hgf(4)], t=[hgf(2), hgf(6)], xch=(lnbf[:, 0:T], r_L[0:1]),
                        xcb=(lnbf[:, 3 * T:3 * T + T // 2].bitcast(BF16), r_L[3:4]),
                        q=[(lnbf[:, T:2 * T], r_L[1:2]), (lnbf[:, 2 * T:3 * T], r_L[2:3])])

        p1_banks = {}

        def p1_A(j, g1):
            v = p1_views(j)
            xch, r_xch = v["xch"]
            xcb, r_xcb = v["xcb"]
            bc, rbc = bank()
            B.group(PE, [r_xrn[j], r_dg4], rbc, [
                (lambda k=k: nc.tensor.matmul(bc, lhsT=dg4[:, j, k, :], rhs=xrn[:, j, HL - 2 + k:HL - 2 + k + T],
                                              start=(k == 0), stop=(k == 3)))
                for k in range(4)])
            B.op(DVE, rbc + [r_pp], r_xcb, lambda: nc.vector.tensor_scalar(
                out=xcb, in0=bc, scalar1=col(pp, CB4 + j), scalar2=None, op0=ALU.add))
            B.op(DVE, rbc + [r_dp], r_xch, lambda: nc.vector.tensor_scalar(
                out=xch, in0=bc, scalar1=0.5, scalar2=col(dp, HCB4 + j), op0=ALU.mult, op1=ALU.add))
            bl = []
            for d in range(2):
                br, rbr = bank()
                bi, rbi = bank()
                B.group(PE, r_xcb + [r_lruw], rbr, [lambda br=br, d=d: nc.tensor.matmul(
                    br, lhsT=lruw[:, 2 * d, j, :], rhs=xcb, start=True, stop=True)])
                B.group(PE, r_xcb + [r_lruw], rbi, [lambda bi=bi, d=d: nc.tensor.matmul(
                    bi, lhsT=lruw[:, 2 * d + 1, j, :], rhs=xcb, start=True, stop=True)])
                bl.append((br, rbr, bi, rbi))
            p1_banks[j] = bl

        def p1_B(j, g1):
            v = p1_views(j)
            xch, r_xch = v["xch"]
            for d in range(2):
                br, rbr, bi, rbi = p1_banks[j][d]
                cdx = d * 8 + j
                aa, raa = v["a"][d]
                qq, rqq = v["q"][d]
                tt, rtt = v["t"][d]
                B.op(ACT, rbr + [r_dp], rbr + [r_SR], lambda br=br, cdx=cdx: nc.scalar.activation(
                    out=br, in_=br, func=AF.Tanh, bias=col(dp, HBA + cdx), scale=0.5,
                    accum_out=SR[:, g1, cdx:cdx + 1]))
                act(aa, br, AF.Exp, rbr + [r_dp], raa, scale=col(dp, HK + cdx), bias=col(dp, HK + cdx))
                B.op(POOL, raa, rqq, lambda qq=qq, aa=aa: nc.gpsimd.tensor_tensor(out=qq, in0=aa, in1=aa, op=ALU.mult))
                B.op(POOL, rqq, rqq, lambda qq=qq: nc.gpsimd.tensor_scalar(
                    out=qq, in0=qq, scalar1=0.99999994, scalar2=-1.0, op0=ALU.min, op1=ALU.max))
                act(bi, bi, AF.Tanh, rbi + [r_dp], rbi, scale=0.5, bias=col(dp, HBX + cdx))
                B.op(DVE, rbi + r_xch, rtt, lambda tt=tt, bi=bi: nc.vector.scalar_tensor_tensor(
                    out=tt, in0=bi, scalar=1.0, in1=xch, op0=ALU.add, op1=ALU.mult))

        def p1_S(j):
            v = p1_views(j)
            for d in range(2):
                qq, rqq = v["q"][d]
                act(qq, qq, AF.Sqrt, rqq, rqq, scale=-1.0, bias=1.0)

        def p1_K(j, g1):
            v = p1_views(j)
            for d in range(2):
                cdx = d * 8 + j
                aa, raa = v["a"][d]
                qq, rqq = v["q"][d]
                tt, rtt = v["t"][d]
                B.op(DVE, rtt + rqq, rtt, lambda tt=tt, qq=qq: nc.vector.tensor_tensor(
                    out=tt, in0=tt, in1=qq, op=ALU.mult))
                if d == 0:
                    B.op(DVE, rtt + raa, rtt, lambda tt=tt, aa=aa: nc.vector.tensor_tensor_scan(
                        out=tt, data0=aa, data1=tt, initial=0.0, op0=ALU.mult, op1=ALU.add))
                else:
                    B.op(DVE, rtt + raa, rtt, lambda tt=tt, aa=aa: nc.vector.tensor_tensor_scan(
                        out=tt[:, ::-1], data0=aa[:, ::-1], data1=tt[:, ::-1], initial=0.0,
                        op0=ALU.mult, op1=ALU.add))
                endc = T - 1 if d == 0 else 0
                B.op(POOL, rtt, [r_BB], lambda tt=tt, endc=endc, cdx=cdx: nc.gpsimd.tensor_copy(
                    out=BBt[:, g1, cdx:cdx + 1], in_=tt[:, endc:endc + 1]))

        def p1_blocks(g1, W, rW, next_tile):
            p1_A(0, g1); p1_B(0, g1); p1_A(1, g1); p1_B(1, g1); p1_A(2, g1)
            for jp in range(0, NB, 2):
                p1_S(jp); p1_S(jp + 1)
                p1_K(jp, g1); p1_K(jp + 1, g1)
                if jp + 2 < NB:
                    p1_B(jp + 2, g1); p1_A(jp + 3, g1); p1_B(jp + 3, g1)
                    if next_tile is not None and jp == 0:
                        stage1(*next_tile)
                    if next_tile is not None and jp + 3 == NB - 1:
                        evc_force[0] = "act"
                        stage2a(W, rW)
                        evc_force[0] = None
                if jp + 4 < NB:
                    p1_A(jp + 4, g1)

        def ln_stats(s, src, rsrc):
            B.op(DVE, rsrc, [r_ln[s]], lambda: nc.vector.bn_stats(out=lst[:, s, 0, :], in_=src[:, 0:512]))
            B.op(DVE, rsrc, [r_ln[s]], lambda: nc.vector.bn_stats(out=lst[:, s, 1, :], in_=src[:, 512:1024]))
            B.op(DVE, [r_ln[s]], [r_ln[s]], lambda: nc.vector.bn_aggr(
                out=lmv[:, s, :], in_=lst[:, s].rearrange("p a b -> p (a b)")))
            B.op(DVE, [r_ln[s]], [r_ln[s]], lambda: nc.vector.tensor_scalar(
                out=lve[:, s:s + 1], in0=lmv[:, s, 1:2], scalar1=EPS, scalar2=None, op0=ALU.add))
            B.op(POOL, [r_ln[s], r_mhalf], [r_ln[s]], lambda: nc.gpsimd.tensor_tensor(
                out=lrs[:, s:s + 1], in0=lve[:, s:s + 1], in1=mhalf[:], op=ALU.pow))
            B.op(DVE, [r_ln[s]], [r_ln[s]], lambda: nc.vector.scalar_tensor_tensor(
                out=lnm[:, s:s + 1], in0=lmv[:, s, 0:1], scalar=-1.0, in1=lrs[:, s:s + 1],
                op0=ALU.mult, op1=ALU.mult))

        W0, rW0 = ws_acquire("win", 0)
        tiles1 = []
        for g in range(NT_OWN):
            seg, i = (g // 8, g % 8) if g < 16 else (2, g - 16)
            tiles1.append((xs, seg * (SEQ + 2 * HL) + i * T))
        for u in range(NT_S):
            tiles1.append((xsamp, u * TC))
        stage1(*tiles1[0])
        stage2a(W0, rW0)
        for g1 in range(NT1):
            p1_blocks(g1, W0, rW0, tiles1[g1 + 1] if g1 + 1 < NT1 else None)
            cast_steps(3)

        cast_steps(len(chunks))
        B.barrier()
        B.dma(SP, B.sem("dc3"), [], [r_lnb], [lambda: nc.sync.dma_start(out=lnb[:], in_=lnb_d)])
        for j in range(NB):
            sj = j % 2
            B.op(POOL, [r_identb, r_dp], [r_dg31[sj]], lambda sj=sj, j=j: nc.gpsimd.tensor_tensor(
                out=dg31[:, sj], in0=identb[:].unsqueeze(1).to_broadcast([P, 31, P]),
                in1=dp[:, HCW31 + j * 31:HCW31 + (j + 1) * 31].unsqueeze(2).to_broadcast([P, 31, P]),
                op=ALU.mult))
            B.dma(POOL, d_dgw[sj], [r_dg31[sj]], [r_dgs[j]], [lambda sj=sj, j=j: nc.gpsimd.dma_start(
                out=dgs[j], in_=dg31[:, sj].rearrange("p k q -> p (k q)"))])

        SRf = SR[:]
        B.op(DVE, [r_SR, r_dp], [r_AA], lambda: nc.vector.tensor_tensor(
            out=AA[:], in0=SRf, in1=dp[:, HK:HK + 16].unsqueeze(1).to_broadcast([P, NT1, 16]), op=ALU.mult))
        B.op(DVE, [r_AA, r_dp], [r_AA], lambda: nc.vector.tensor_tensor(
            out=AA[:], in0=AA[:], in1=dp[:, K256:K256 + 16].unsqueeze(1).to_broadcast([P, NT1, 16]), op=ALU.add))
        act(AA[:], AA[:], AF.Exp, [r_AA], [r_AA])
        for d in range(2):
            sl = slice(d * 8, d * 8 + 8)
            mk = msk[:, 2 * d, :].unsqueeze(2).to_broadcast([P, NT_S, 8])
            omk = msk[:, 2 * d + 1, :].unsqueeze(2).to_broadcast([P, NT_S, 8])
            As = AA[:, NT_OWN:NT1, sl]
            Bs = BBt[:, NT_OWN:NT1, sl]
            B.op(DVE, [r_AA, r_msk], [r_AA], lambda As=As, mk=mk: nc.vector.tensor_tensor(out=As, in0=As, in1=mk, op=ALU.mult))
            B.op(DVE, [r_AA, r_msk], [r_AA], lambda As=As, omk=omk: nc.vector.tensor_tensor(out=As, in0=As, in1=omk, op=ALU.add))
            B.op(DVE, [r_BB, r_msk], [r_BB], lambda Bs=Bs, mk=mk: nc.vector.tensor_tensor(out=Bs, in0=Bs, in1=mk, op=ALU.mult))
        HFc = fold[:, 0, 0:8]
        HBc = fold[:, 0, 8:16]
        tmpc = fold[:, 1, 0:8]
        B.op(DVE, [], [r_fold], lambda: nc.vector.memset(fold[:], 0.0))

        def step(Hout, Hin, g, sl, rH_in, rH_out):
            B.op(DVE, [r_AA] + rH_in, [r_fold], lambda: nc.vector.tensor_tensor(
                out=tmpc, in0=AA[:, g, sl], in1=Hin, op=ALU.mult))
            B.op(DVE, [r_fold, r_BB], rH_out, lambda: nc.vector.tensor_tensor(
                out=Hout, in0=tmpc, in1=BBt[:, g, sl], op=ALU.add))

        fs, bsl = slice(0, 8), slice(8, 16)
        for u in range(NT_S):
            step(HFc, HFc, NT_OWN + u, fs, [r_fold], [r_fold])
        for u in range(NT_S - 1, -1, -1):
            step(HBc, HBc, NT_OWN + u, bsl, [r_fold], [r_fold])
        B.op(DVE, [r_fold], [r_CAR], lambda: nc.vector.tensor_copy(out=CAR[:, 16, 0:8], in_=HFc))
        B.op(DVE, [r_fold], [r_CAR], lambda: nc.vector.tensor_copy(out=CAR[:, 19, 8:16], in_=HBc))
        for (g0, n) in ((0, 8), (8, 8), (16, 4)):
            for g in range(g0, g0 + n - 1):
                step(CAR[:, g + 1, 0:8], CAR[:, g, 0:8], g, fs, [r_CAR], [r_CAR])
            for g in range(g0 + n - 1, g0, -1):
                step(CAR[:, g - 1, 8:16], CAR[:, g, 8:16], g, bsl, [r_CAR], [r_CAR])

        evc_dve_only[0] = False
        mg = xrn
        r_mg = r_xrn
        x1T = hg
        r_x1T = r_hg
        stage1(*tiles1[0])
        W, rW = ws_acquire("win", 0)
        stage2a(W, rW)
        for g in range(NT_OWN):
            src, ext0 = tiles1[g]
            out0 = g * T
            W, rW = ws_acquire("win", 1)
            Wv = W[:].rearrange("p (k c) -> p k c", c=D)
            for j in range(NB):
                bk, rb = bank()
                B.group(PE, r_xT + rW, rb, [
                    (lambda kc=kc, j=j, bk=bk, Wv=Wv: nc.tensor.matmul(
                        bk, lhsT=Wv[:, kc, j * P:(j + 1) * P], rhs=xT[:, kc, HL:HL + T],
                        start=(kc == 0), stop=(kc == NB - 1)))
                    for kc in range(NB)])
                act(Y[:, j, :], bk, AF.Gelu_apprx_tanh, rb, [r_Y[j]])
            Wc, rWc = ws_acquire("win", 3)
            Wu, rWu = ws_acquire("win", 2, hold=1)
            Wcv = Wc[:].rearrange("p (k c) -> p k c", c=D)
            Wuv = Wu[:].rearrange("p (k c) -> p k c", c=D)
            def s3_diag(j):
                s = j % 2
                B.dma(SP, d_dgr[s], [r_dgs[j]], [r_dg31[s]], [lambda s=s, j=j: nc.sync.dma_start(
                    out=dg31[:, s].rearrange("p k q -> p (k q)"), in_=dgs[j])])

            def s3_proj(j):
                s = j % 2
                sig, r_sig = zf(22 + 2 * s)
                uflat, r_u = zb(26 + 2 * s, 2)
                for h in range(2):
                    bv, rbv = bank()
                    bu, rbu = bank()
                    B.group(PE, r_xT + rWc, rbv, [
                        (lambda kc=kc, bv=bv, h=h, j=j: nc.tensor.matmul(
                            bv[:, 0:HC], lhsT=Wcv[:, kc, j * P:(j + 1) * P], rhs=xT[:, kc, h * HC:(h + 1) * HC],
                            start=(kc == 0), stop=(kc == NB - 1)))
                        for kc in range(NB)])
                    B.group(PE, r_xT + rWu, rbu, [
                        (lambda kc=kc, bu=bu, h=h, j=j: nc.tensor.matmul(
                            bu[:, 0:HC], lhsT=Wuv[:, kc, j * P:(j + 1) * P], rhs=xT[:, kc, h * HC:(h + 1) * HC],
                            start=(kc == 0), stop=(kc == NB - 1)))
                        for kc in range(NB)])
                    act(sig[:, 0:HC], bv[:, 0:HC], AF.Tanh, rbv, r_sig, scale=0.5)
                    B.op(DVE, r_sig + rbu, r_u, lambda h=h, bu=bu, sig=sig, uflat=uflat: nc.vector.scalar_tensor_tensor(
                        out=uflat[:, h * HC:(h + 1) * HC], in0=sig[:, 0:HC], scalar=1.0, in1=bu[:, 0:HC],
                        op0=ALU.add, op1=ALU.mult))

            def s3_conv(j):
                s = j % 2
                uflat, r_u = zb(26 + 2 * s, 2)
                bc, rbc = bank()
                B.group(PE, r_u + [r_dg31[s]], rbc, [
                    (lambda k=k, bc=bc, s=s, uflat=uflat: nc.tensor.matmul(
                        bc, lhsT=dg31[:, s, k, :], rhs=uflat[:, 1 + k:1 + k + T], start=(k == 0), stop=(k == 30)))
                    for k in range(31)])
                act(Y[:, j, :], bc, AF.Identity, rbc + [r_pp], [r_Y[j]], scale=1.0, bias=col(pp, CB31 + j))

            ps_split[0] = True
            for j in range(NB):
                s3_diag(j)
                stage2b(j, None, True, g, mid=lambda j=j: s3_proj(j))
                if j >= 1:
                    s3_conv(j - 1)
            s3_conv(NB - 1)
            ps_split[0] = False
            for s in range(4):
                pr, rpr = pair()
                B.group(PE, r_Y + [r_ident], rpr, [
                    (lambda j=j, s=s, pr=pr: nc.tensor.transpose(
                        out=pr[:, j * P:(j + 1) * P], in_=Y[:, j, s * P:(s + 1) * P], identity=ident[:]))
                    for j in range(NB)])
                ln_stats(s, pr, rpr)
                act(tokbuf[:, s, :], pr, AF.Identity, rpr + [r_ln[s]], [r_tok[s]],
                    scale=lrs[:, s:s + 1], bias=lnm[:, s:s + 1])
            for j in range(NB):
                bk, rb = bank()
                B.group(PE, r_tok + [r_ident], rb, [
                    (lambda s=s, j=j, bk=bk: nc.tensor.transpose(
                        out=bk[:, s * P:(s + 1) * P], in_=tokbuf[:, s, j * P:(j + 1) * P], identity=ident[:]))
                    for s in range(4)])
                act(ub[:, j, :], bk, AF.Silu, rb + [r_pp], [r_ub[j]], scale=col(pp, CNG + j), bias=col(pp, CNB + j))
            for gq in range(4):
                W, rW = ws_acquire("mix", gq)
                Wm = W[:].rearrange("p (m k c) -> p m k c", m=4, k=NB)
                for jj in range(2):
                    j = 2 * gq + jj
                    s = j % 2
                    cs = slice(jj * P, (jj + 1) * P)
                    bya, rya = bank()
                    byb, ryb = bank()
                    bga, rga = bank()
                    bgb, rgb = bank()
                    for (bk, rb, m, rhs_t, rr, off) in ((bya, rya, 0, hg, r_hg, 0), (byb, ryb, 1, ub, r_ub, 0),
                                                     (bga, rga, 2, xT, r_xT, HL), (bgb, rgb, 3, xT, r_xT, HL)):
                        B.group(PE, rr + rW, rb, [
                            (lambda kc=kc, bk=bk, m=m, rhs_t=rhs_t, off=off, cs=cs, Wm=Wm: nc.tensor.matmul(
                                bk, lhsT=Wm[:, m, kc, cs], rhs=rhs_t[:, kc, off:off + T],
                                start=(kc == 0), stop=(kc == NB - 1)))
                            for kc in range(NB)])
                    ga, r_ga = zf(6 * s)
                    gb, r_gb = zf(6 * s + 2)
                    t1, r_t1 = zf(6 * s + 4)
                    act(ga, bga, AF.Sigmoid, rga + [r_pp], r_ga, bias=col(pp, GB + j))
                    act(gb, bgb, AF.Sigmoid, rgb + [r_pp], r_gb, bias=col(pp, GB + 8 + j))
                    B.op(DVE, r_ga + rya, r_t1, lambda t1=t1, ga=ga, bya=bya: nc.vector.tensor_tensor(
                        out=t1, in0=ga, in1=bya, op=ALU.mult))
                    B.op(DVE, r_gb + ryb, ryb, lambda gb=gb, byb=byb: nc.vector.tensor_tensor(
                        out=byb, in0=gb, in1=byb, op=ALU.mult))
                    B.op(DVE, r_t1 + ryb, [r_mg[j]], lambda t1=t1, byb=byb, j=j: nc.vector.tensor_tensor(
                        out=mg[:, j, 0:T], in0=t1, in1=byb, op=ALU.add))
            W, rW = ws_acquire("wout", 0)
            Wo = W[:].rearrange("p (k c) -> p k c", c=D)
            load_tok(src, ext0 + HL)
            for s in range(4):
                pr, rpr = pair()
                for hf in range(2):
                    B.group(PE, r_mg + rW, [rpr[hf]], [
                        (lambda kc=kc, s=s, hf=hf, pr=pr, Wo=Wo: nc.tensor.matmul(
                            pr[:, hf * 512:(hf + 1) * 512], lhsT=mg[:, kc, s * P:(s + 1) * P],
                            rhs=Wo[:, kc, hf * 512:(hf + 1) * 512], start=(kc == 0), stop=(kc == NB - 1)))
                        for kc in range(NB)])
                B.op(DVE, [r_tok[s]] + rpr, [r_tok[s]], lambda s=s, pr=pr: nc.vector.scalar_tensor_tensor(
                    out=tokbuf[:, s, :], in0=tokbuf[:, s, :], scalar=ALPHA, in1=pr, op0=ALU.mult, op1=ALU.add))
                ln_stats(s, tokbuf[:, s, :], [r_tok[s]])
                act(tokbuf[:, s, :], tokbuf[:, s, :], AF.Identity, [r_tok[s], r_ln[s]], [r_tok[s]],
                    scale=lrs[:, s:s + 1], bias=lnm[:, s:s + 1])
            if g + 1 < NT_OWN:
                stage1(*tiles1[g + 1], use_y=True)
                Wn, rWn = ws_acquire("win", 0)
                stage2a(Wn, rWn)
            for j in range(NB):
                bk, rb = bank()
                B.group(PE, r_tok + [r_ident], rb, [
                    (lambda s=s, j=j, bk=bk: nc.tensor.transpose(
                        out=bk[:, s * P:(s + 1) * P], in_=tokbuf[:, s, j * P:(j + 1) * P], identity=ident[:]))
                    for s in range(4)])
                act(x1T[:, j, :], bk, AF.Identity, rb + [r_pp], [r_x1T[j]], scale=col(pp, L1G + j), bias=col(pp, L1B + j))
                act(Y[:, j, :], bk, AF.Identity, rb + [r_dp], [r_Y[j]], scale=col(dp, AG1 + j), bias=col(dp, AB1 + j))
            for gq in range(6):
                W, rW = ws_acquire("ffi", gq)
                Wf = W[:].rearrange("p (m k c) -> p m k c", m=2, k=NB)
                nblk = min(4, NFB - 4 * gq)
                for b in range(nblk):
                    f = 4 * gq + b
                    cs = slice(b * P, (b + 1) * P)
                    bg_, rbg = bank()
                    bu, rbu = bank()
                    for (bk, rb, m) in ((bg_, rbg, 0), (bu, rbu, 1)):
                        B.group(PE, r_x1T + rW, rb, [
                            (lambda kc=kc, bk=bk, m=m, cs=cs, Wf=Wf: nc.tensor.matmul(
                                bk, lhsT=Wf[:, m, kc, cs], rhs=x1T[:, kc, :], start=(kc == 0), stop=(kc == NB - 1)))
                            for kc in range(NB)])
                    sg, r_sg = zf(22 + 2 * (f % 2))
                    act(sg, bg_, AF.Silu, rbg, r_sg)
                    hT, r_hT = zb(f)
                    B.op(DVE, r_sg + rbu, r_hT, lambda hT=hT, sg=sg, bu=bu: nc.vector.tensor_tensor(
                        out=hT, in0=sg, in1=bu, op=ALU.mult))
            for gq in range(4):
                W, rW = ws_acquire("ffo", gq)
                Wq = W[:, 0:NFB * 256].rearrange("p (f c) -> p f c", c=256)
                for jj in range(2):
                    j = 2 * gq + jj
                    bk, rb = bank()
                    B.group(PE, r_Z[0:NFB] + rW, rb, [
                        (lambda f=f, bk=bk, jj=jj, Wq=Wq: nc.tensor.matmul(
                            bk, lhsT=Wq[:, f, jj * P:(jj + 1) * P], rhs=Z[:, f, :], start=(f == 0), stop=(f == NFB - 1)))
                        for f in range(NFB)])
                    B.op(DVE, [r_Y[j]] + rb, [r_Y[j]], lambda j=j, bk=bk: nc.vector.tensor_tensor(
                        out=Y[:, j, :], in0=Y[:, j, :], in1=bk, op=ALU.add))
            for s in range(4):
                pr, rpr = pair()
                B.group(PE, r_Y + [r_ident], rpr, [
                    (lambda j=j, s=s, pr=pr: nc.tensor.transpose(
                        out=pr[:, j * P:(j + 1) * P], in_=Y[:, j, s * P:(s + 1) * P], identity=ident[:]))
                    for j in range(NB)])
                ln_stats(s, pr, rpr)
                act(tokbuf[:, s, :], pr, AF.Identity, rpr + [r_ln[s]], [r_tok[s]],
                    scale=lrs[:, s:s + 1], bias=lnm[:, s:s + 1])
                B.op(DVE, [r_tok[s], r_lnb], [r_tok[s]], lambda s=s: nc.vector.tensor_tensor(
                    out=tokbuf[:, s, :], in0=tokbuf[:, s, :], in1=lnb[:, 0, :], op=ALU.mult))
                B.op(DVE, [r_tok[s], r_lnb], [r_tok[s]], lambda s=s: nc.vector.tensor_tensor(
                    out=tokbuf[:, s, :], in0=tokbuf[:, s, :], in1=lnb[:, 1, :], op=ALU.add))
                B.dma(POOL, d_to[s], [r_tok[s]], [r_y], [lambda s=s, out0=out0: nc.gpsimd.dma_start(
                    out=y[out0 + s * P:out0 + (s + 1) * P, :], in_=tokbuf[:, s, :])])

        B._wait(SP, [(d, d.n) for d in d_to])
        B._wait(POOL, [(d, d.n) for d in d_to])
    return nc


_NC = None


def _prep_shared(inp):
    f = lambda a: np.ascontiguousarray(np.asarray(a, dtype=np.float32))
    ch = lambda v: f(v).reshape(NB, P).T
    pp = np.zeros((P, NPP), np.float32)
    pp[:, CW4:CW4 + 32] = f(inp["rnn_conv_w"])[0].reshape(4, NB, P).transpose(2, 1, 0).reshape(P, 32)
    pp[:, CB4:CB4 + 8] = ch(inp["rnn_conv_b"][0])
    for d in range(2):
        pp[:, BA + d * 8:BA + d * 8 + 8] = ch(inp["lru_ba"][0][d])
        pp[:, BX + d * 8:BX + d * 8 + 8] = ch(inp["lru_bx"][0][d])
        pp[:, LAM + d * 8:LAM + d * 8 + 8] = ch(inp["lru_lambda"][0][d])
        pp[:, GB + d * 8:GB + d * 8 + 8] = ch(inp["gate_b"][0][d])
    pp[:, CW31:CW31 + 248] = f(inp["conf_conv_w"])[0].reshape(31, NB, P).transpose(2, 1, 0).reshape(P, 248)
    pp[:, CB31:CB31 + 8] = ch(inp["conf_conv_b"][0])
    pp[:, CNG:CNG + 8] = ch(inp["conf_norm_g"][0])
    pp[:, CNB:CNB + 8] = ch(inp["conf_norm_b"][0])
    pp[:, L1G:L1G + 8] = ch(inp["ln1_g"][0])
    pp[:, L1B:L1B + 8] = ch(inp["ln1_b"][0])
    wa = f(inp["lru_wa"])[0]
    wx = f(inp["lru_wx"])[0]
    lw = np.stack([wa[0], wx[0], wa[1], wx[1]], axis=0)
    lruw = np.ascontiguousarray(lw.transpose(2, 0, 1, 3)).reshape(P, 4 * NB * P)
    lnb = np.ascontiguousarray(np.broadcast_to(
        np.stack([f(inp["ln2_g"])[0], f(inp["ln2_b"])[0]], axis=0)[None], (P, 2, D)))
    return dict(
        w_in=f(inp["w_in"])[0], w_rp=f(inp["w_rnn_proj"])[0], w_cp=f(inp["w_conf_proj"])[0], w_o=f(inp["w_out"])[0],
        w_fi=f(inp["w_ffn_in"])[0], w_fo=f(inp["w_ffn_out"])[0], lruw=lruw, pp=pp, lnb=lnb,
        idn=np.eye(P, dtype=np.float32))


def kernel(**inputs):
    global _NC
    xp = np.asarray(inputs["x_prompt"], dtype=np.float32)
    xsm = np.asarray(inputs["x_sample"], dtype=np.float32)[0]
    shared = _prep_shared(inputs)
    xsamp = np.zeros((SAMP + 2 * HL, D), np.float32)
    xsamp[HL:HL + SAMP] = xsm
    xwin = np.stack([xsamp[u * T:u * T + TC] for u in range(SAMP // T)], axis=0)
    in_maps = []
    for c in range(8):
        xs = np.zeros((XS_ROWS, D), np.float32)
        for q in range(2):
            r0 = q * (SEQ + 2 * HL)
            xs[r0 + HL:r0 + HL + SEQ] = xp[2 * c + q]
        r0 = 2 * (SEQ + 2 * HL)
        xs[r0:r0 + CH + 2 * HL] = xsamp[c * CH:c * CH + CH + 2 * HL]
        m = np.zeros((P, 4, NT_S), np.float32)
        u = np.arange(NT_S)
        mf = (u < 4 * c).astype(np.float32)
        mb = (u >= 4 * c).astype(np.float32)
        m[:, 0, :] = mf
        m[:, 1, :] = 1.0 - mf
        m[:, 2, :] = mb
        m[:, 3, :] = 1.0 - mb
        d = dict(shared)
        d["xs"] = xs
        others = [t for t in range(SAMP // T) if not (4 * c <= t < 4 * c + 4)]
        d["xsamp"] = np.ascontiguousarray(xwin[others]).reshape(NT_S * TC, D)
        d["msk"] = m
        in_maps.append(d)
    if _NC is None:
        _NC = build_program()
    res = run_bass_kernel_spmd(_NC, in_maps, core_ids=list(range(8)))
    y_prompt = np.zeros((16, SEQ, D), np.float32)
    y_sample = np.zeros((1, SAMP, D), np.float32)
    for c in range(8):
        yc = np.asarray(res.results[c]["y"], dtype=np.float32)
        y_prompt[2 * c] = yc[0:SEQ]
        y_prompt[2 * c + 1] = yc[SEQ:2 * SEQ]
        y_sample[0, c * CH:(c + 1) * CH] = yc[2 * SEQ:2 * SEQ + CH]
    return (y_prompt, y_sample)
```
